# Optimizing a Trainium2 kernel written in Bass

```python
import math
import jax, jax.numpy as jnp
from jax import lax
import numpy as np

D_MODEL = 1024
BATCH = 2
SEQ = 8192
DEPTH = 2

GRID_W = 64
CTX_LEN = 256
NORM_EPS = 1e-6

ATT_HEADS = 8
ATT_KV_HEADS = 2
HEAD_DIM = 64
ATT_WIDTH = ATT_HEADS * HEAD_DIM
KV_WIDTH = ATT_KV_HEADS * HEAD_DIM
ROPE_THETA = 10000.0
ROPE_FREQS = HEAD_DIM // 4
Q_BLOCK = 128

CONV_WIDTH = 512

POOL_WIDTH = 512
POOL_WINDOWS = (2, 4, 8, 16)
POOL_GROUP = POOL_WIDTH // 4

SSD_HEADS = 8
SSD_HEAD_DIM = 64
SSD_WIDTH = SSD_HEADS * SSD_HEAD_DIM
SSD_GROUPS = 2
SSD_STATE = 64
SSD_CHUNK = 128
SSD_CONV_CH = SSD_WIDTH + 2 * SSD_GROUPS * SSD_STATE
DT_MIN = 1e-3
DT_MAX = 1e-1

N_BRANCH = 4
BRANCH_WIDTH = 512

D_FF = 2816

IN_SPLITS = (ATT_WIDTH, KV_WIDTH, KV_WIDTH,
             CONV_WIDTH, CONV_WIDTH, CONV_WIDTH,
             POOL_WIDTH,
             SSD_WIDTH, SSD_CONV_CH, 2 * SSD_HEADS,
             N_BRANCH * D_MODEL)
IN_DIM = 768 + 1536 + 512 + 512 + 768 + 16 + 4096

kernel_name = "hybrid_parallel_gqa_conv_pool_ssd_adaln"

F32 = jnp.float32


def rms_norm(x, g):
    xf = x.astype(F32)
    y = xf * lax.rsqrt(jnp.mean(xf * xf, axis=-1, keepdims=True) + NORM_EPS)
    return (y * g.astype(F32)).astype(x.dtype)


def modulation(cond, w_mod, b_mod):
    m = jax.nn.silu(cond) @ w_mod + b_mod
    return jnp.split(m[..., None, :], 6, axis=-1)


def modulate(h, shift, scale):
    return h * (1 + scale) + shift


def conv3_centred(u, w):
    up = jnp.pad(u, ((0, 0), (1, 1), (0, 0)))
    return up[:, :-2] * w[0] + up[:, 1:-1] * w[1] + up[:, 2:] * w[2]


def split_in(proj):
    idx = [int(i) for i in np.cumsum(IN_SPLITS)[:-1]]
    return jnp.split(proj, idx, axis=-1)


def axial_rope_tables(n_tokens):
    rows = n_tokens // GRID_W
    row = jnp.repeat(jnp.arange(rows), GRID_W).astype(F32)
    col = jnp.tile(jnp.arange(GRID_W), rows).astype(F32)
    inv = ROPE_THETA ** (-jnp.arange(ROPE_FREQS, dtype=F32) / ROPE_FREQS)
    ang = jnp.stack([row[:, None] * inv, col[:, None] * inv], axis=1)
    return jnp.cos(ang), jnp.sin(ang)


def apply_axial_rope(t, cos, sin):
    b, n, h, d = t.shape
    tf = t.astype(F32).reshape(b, n, h, 2, 2, ROPE_FREQS)
    t1, t2 = tf[..., 0, :], tf[..., 1, :]
    cs, sn = cos[None, :, None], sin[None, :, None]
    out = jnp.stack([t1 * cs - t2 * sn, t1 * sn + t2 * cs], axis=-2)
    return out.reshape(b, n, h, d).astype(t.dtype)


def split_heads(t, n_heads):
    return t.reshape(t.shape[0], t.shape[1], n_heads, HEAD_DIM)


def gqa_attend(q, k, v):
    b, nq, h, dh = q.shape
    g = k.shape[2]
    qg = q.reshape(b, nq, g, h // g, dh)
    s = jnp.einsum('bqgrd,bkgd->bgrqk', qg, k).astype(F32) * (dh ** -0.5)
    p = jax.nn.softmax(s, axis=-1).astype(v.dtype)
    o = jnp.einsum('bgrqk,bkgd->bqgrd', p, v)
    return o.reshape(b, nq, h, dh)


def attention_branch(q, k, v, qc, kc, vc, q_norm, k_norm, with_ctx_out):
    b, n, _ = q.shape
    cos, sin = axial_rope_tables(n)
    q = apply_axial_rope(rms_norm(split_heads(q, ATT_HEADS), q_norm), cos, sin)
    k = apply_axial_rope(rms_norm(split_heads(k, ATT_KV_HEADS), k_norm), cos, sin)
    v = split_heads(v, ATT_KV_HEADS)
    kc = rms_norm(split_heads(kc, ATT_KV_HEADS), k_norm)
    vc = split_heads(vc, ATT_KV_HEADS)
    keys = jnp.concatenate([kc, k], axis=1)
    vals = jnp.concatenate([vc, v], axis=1)
    qb = q.reshape(b, n // Q_BLOCK, Q_BLOCK, ATT_HEADS, HEAD_DIM).transpose(1, 0, 2, 3, 4)
    out = lax.map(lambda blk: gqa_attend(blk, keys, vals), qb)
    out = out.transpose(1, 0, 2, 3, 4).reshape(b, n, ATT_WIDTH)
    out_c = None
    if with_ctx_out:
        qch = rms_norm(split_heads(qc, ATT_HEADS), q_norm)
        out_c = gqa_attend(qch, kc, vc).reshape(b, qc.shape[1], ATT_WIDTH)
    return out, out_c


def short_conv_branch(b_gate, c_gate, u, w_conv):
    return b_gate * conv3_centred(c_gate * u, w_conv)


def centred_pool_minus_self(u, window):
    n = u.shape[1]
    uf = u.astype(F32)
    cs = jnp.pad(jnp.cumsum(uf, axis=1), ((0, 0), (1, 0), (0, 0)))
    t = jnp.arange(n)
    lo = jnp.clip(t - window // 2, 0, n)
    hi = jnp.clip(t + window // 2, 0, n)
    mean = (cs[:, hi] - cs[:, lo]) / (hi - lo).astype(F32)[None, :, None]
    return (mean - uf).astype(u.dtype)


def pool_branch(u, w_pool, pool_scale):
    b, n, _ = u.shape
    groups = jnp.split(u, len(POOL_WINDOWS), axis=-1)
    pooled = jnp.stack([centred_pool_minus_self(g, w) for g, w in zip(groups, POOL_WINDOWS)], axis=-2)
    mixed = jnp.einsum('blgc,gcd->blgd', pooled, w_pool).reshape(b, n, POOL_WIDTH)
    return mixed * pool_scale


def segsum(a):
    t = a.shape[-1]
    rep = jnp.broadcast_to(a[..., :, None], a.shape + (t,))
    rep = jnp.where(jnp.tril(jnp.ones((t, t), bool), -1), rep, 0.0)
    s = jnp.cumsum(rep, axis=-2)
    return jnp.where(jnp.tril(jnp.ones((t, t), bool), 0), s, -jnp.inf)


def ssd_scan(x, dt, a_rate, bm, cm, h0):
    b, l, nh, p = x.shape
    g, n = bm.shape[2], bm.shape[3]
    r = nh // g
    nc, tt = l // SSD_CHUNK, SSD_CHUNK
    xdt = (x.astype(F32) * dt[..., None]).reshape(b, nc, tt, g, r, p)
    bc = bm.astype(F32).reshape(b, nc, tt, g, n)
    cc = cm.astype(F32).reshape(b, nc, tt, g, n)
    a = (dt * a_rate).reshape(b, nc, tt, g, r).transpose(0, 1, 3, 4, 2)
    a_cum = jnp.cumsum(a, axis=-1)
    cb = jnp.einsum('bclgn,bcsgn->bcgls', cc, bc)
    w = cb[:, :, :, None] * jnp.exp(segsum(a))
    y_diag = jnp.einsum('bcgrls,bcsgrp->bclgrp', w, xdt)
    decay_to_end = jnp.exp(a_cum[..., -1:] - a_cum)
    states = jnp.einsum('bcsgn,bcgrs,bcsgrp->bcgrpn', bc, decay_to_end, xdt)
    chunk_decay = jnp.exp(a_cum[..., -1])

    def step(h, inp):
        st, dec = inp
        return h * dec[..., None, None] + st, h

    h_final, h_prev = lax.scan(step, h0.reshape(b, g, r, p, n),
                               (jnp.moveaxis(states, 1, 0), jnp.moveaxis(chunk_decay, 1, 0)))
    h_prev = jnp.moveaxis(h_prev, 0, 1)
    y_off = jnp.einsum('bclgn,bcgrpn,bcgrl->bclgrp', cc, h_prev, jnp.exp(a_cum))
    y = (y_diag + y_off).reshape(b, l, nh, p)
    return y.astype(x.dtype), h_final.reshape(b, nh, p, n)


def ssd_prep(xbc, dt_raw, conv_w, conv_b, dt_bias):
    b, l, _ = xbc.shape
    xbc = jax.nn.silu(conv3_centred(xbc, conv_w) + conv_b)
    xs, bm, cm = jnp.split(xbc, [SSD_WIDTH, SSD_WIDTH + SSD_GROUPS * SSD_STATE], axis=-1)
    xs = xs.reshape(b, l, SSD_HEADS, SSD_HEAD_DIM)
    bm = bm.reshape(b, l, SSD_GROUPS, SSD_STATE)
    cm = cm.reshape(b, l, SSD_GROUPS, SSD_STATE)
    dt = jax.nn.softplus(dt_raw.astype(F32).reshape(b, l, 2, SSD_HEADS) + dt_bias.astype(F32))
    return xs, bm, cm, dt


def ssd_branch(z, xbc, dt_raw, zc, xbcc, dtc_raw, conv_w, conv_b, dt_bias, a_log, d_skip, norm_g,
               with_ctx_out):
    xs, bm, cm, dt = ssd_prep(xbc, dt_raw, conv_w, conv_b, dt_bias)
    xs_c, bm_c, cm_c, dt_c = ssd_prep(xbcc, dtc_raw, conv_w, conv_b, dt_bias)
    a_rate = -jnp.exp(a_log.astype(F32))
    b = xs.shape[0]
    y = d_skip[:, None] * xs
    y_c = d_skip[:, None] * xs_c
    for d, flip in enumerate((False, True)):
        f = (lambda t: jnp.flip(t, axis=1)) if flip else (lambda t: t)
        h0 = jnp.zeros((b, SSD_HEADS, SSD_HEAD_DIM, SSD_STATE), F32)
        yd_c, h_ctx = ssd_scan(f(xs_c), f(dt_c[:, :, d]), a_rate[d], f(bm_c), f(cm_c), h0)
        yd, _ = ssd_scan(f(xs), f(dt[:, :, d]), a_rate[d], f(bm), f(cm), h_ctx)
        y = y + f(yd)
        y_c = y_c + f(yd_c)
    out = rms_norm(y.reshape(z.shape) * jax.nn.silu(z), norm_g)
    out_c = rms_norm(y_c.reshape(zc.shape) * jax.nn.silu(zc), norm_g) if with_ctx_out else None
    return out, out_c


def merge_branches(branches, gate_logits, w_branch):
    b, n, _ = gate_logits.shape
    gates = jax.nn.sigmoid(gate_logits.astype(F32)).astype(gate_logits.dtype).reshape(b, n, N_BRANCH, D_MODEL)
    proj = jnp.einsum('blkc,kcd->blkd', jnp.stack(branches, axis=-2), w_branch)
    return jnp.sum(gates * proj, axis=-2)


def mixing_sublayer(h, hc, w_in, q_norm, k_norm, conv_short, pool_w, pool_scale,
                    ssd_conv_w, ssd_conv_b, ssd_dt_bias, ssd_a_log, ssd_d, ssd_norm,
                    w_branch, w_out, with_ctx_out):
    q, k, v, bg, cg, u, pu, z, xbc, dt, gl = split_in(h @ w_in)
    qc, kc, vc, bgc, cgc, uc, puc, zc, xbcc, dtc, glc = split_in(hc @ w_in)
    att, att_c = attention_branch(q, k, v, qc, kc, vc, q_norm, k_norm, with_ctx_out)
    ssd, ssd_c = ssd_branch(z, xbc, dt, zc, xbcc, dtc, ssd_conv_w, ssd_conv_b, ssd_dt_bias,
                            ssd_a_log, ssd_d, ssd_norm, with_ctx_out)
    y = merge_branches((att, short_conv_branch(bg, cg, u, conv_short),
                        pool_branch(pu, pool_w, pool_scale), ssd), gl, w_branch) @ w_out
    if not with_ctx_out:
        return y, None
    yc = merge_branches((att_c, short_conv_branch(bgc, cgc, uc, conv_short),
                         pool_branch(puc, pool_w, pool_scale), ssd_c), glc, w_branch) @ w_out
    return y, yc


def conv_ffn(h, w_up, w_conv, w_down):
    up = conv3_centred(h @ w_up, w_conv)
    g, v = jnp.split(up, 2, axis=-1)
    return (jax.nn.silu(g) * v) @ w_down


def setup_inputs(seed: int = 0) -> dict:
    key = jax.random.key(seed)
    ks = iter(jax.random.split(key, 40))

    def nrm(shape, scale):
        return jax.random.normal(next(ks), shape, F32) * scale

    def gain(shape):
        return 1.0 + 0.02 * jax.random.normal(next(ks), shape, F32)

    u = jax.random.uniform(next(ks), (DEPTH, 2, SSD_HEADS), F32)
    dt0 = jnp.exp(u * (math.log(DT_MAX) - math.log(DT_MIN)) + math.log(DT_MIN))
    ssd_dt_bias = dt0 + jnp.log(-jnp.expm1(-dt0))
    ssd_a_log = jnp.log(jax.random.uniform(next(ks), (DEPTH, 2, SSD_HEADS), F32, 1.0, 16.0))
    return {
        "x": nrm((BATCH, SEQ, D_MODEL), 1.0),
        "c": nrm((BATCH, D_MODEL), 1.0),
        "ctx": nrm((BATCH, CTX_LEN, D_MODEL), 1.0),
        "c_ctx": nrm((D_MODEL,), 1.0),
        "w_mod": nrm((DEPTH, D_MODEL, 6 * D_MODEL), 0.5 * D_MODEL ** -0.5),
        "b_mod": nrm((DEPTH, 6 * D_MODEL), 0.02),
        "norm_mix": gain((DEPTH, D_MODEL)),
        "w_in": nrm((DEPTH, D_MODEL, IN_DIM), D_MODEL ** -0.5),
        "q_norm": gain((DEPTH, HEAD_DIM)),
        "k_norm": gain((DEPTH, HEAD_DIM)),
        "conv_short": nrm((DEPTH, 3, CONV_WIDTH), 3 ** -0.5),
        "pool_w": nrm((DEPTH, len(POOL_WINDOWS), POOL_GROUP, POOL_GROUP), POOL_GROUP ** -0.5),
        "pool_scale": 1.0 + 0.1 * jax.random.normal(next(ks), (DEPTH, POOL_WIDTH), F32),
        "ssd_conv_w": nrm((DEPTH, 3, SSD_CONV_CH), 3 ** -0.5),
        "ssd_conv_b": nrm((DEPTH, SSD_CONV_CH), 0.02),
        "ssd_dt_bias": ssd_dt_bias,
        "ssd_a_log": ssd_a_log,
        "ssd_d": gain((DEPTH, SSD_HEADS)),
        "ssd_norm": gain((DEPTH, SSD_WIDTH)),
        "w_branch": nrm((DEPTH, N_BRANCH, BRANCH_WIDTH, D_MODEL), BRANCH_WIDTH ** -0.5),
        "w_out": nrm((DEPTH, D_MODEL, D_MODEL), D_MODEL ** -0.5),
        "norm_ffn": gain((DEPTH, D_MODEL)),
        "w_up": nrm((DEPTH, D_MODEL, 2 * D_FF), D_MODEL ** -0.5),
        "ffn_conv": nrm((DEPTH, 3, 2 * D_FF), 3 ** -0.5),
        "w_down": nrm((DEPTH, D_FF, D_MODEL), D_FF ** -0.5),
        "final_norm": gain((D_MODEL,)),
    }


def reference(x, c, ctx, c_ctx, w_mod, b_mod, norm_mix, w_in, q_norm, k_norm, conv_short,
              pool_w, pool_scale, ssd_conv_w, ssd_conv_b, ssd_dt_bias, ssd_a_log, ssd_d,
              ssd_norm, w_branch, w_out, norm_ffn, w_up, ffn_conv, w_down, final_norm):
    xc = ctx
    for i in range(DEPTH):
        with_ctx_out = i < DEPTH - 1
        sh1, sc1, g1, sh2, sc2, g2 = modulation(c, w_mod[i], b_mod[i])
        csh1, csc1, cg1, csh2, csc2, cg2 = modulation(c_ctx, w_mod[i], b_mod[i])
        h = modulate(rms_norm(x, norm_mix[i]), sh1, sc1)
        hc = modulate(rms_norm(xc, norm_mix[i]), csh1, csc1)
        y, yc = mixing_sublayer(h, hc, w_in[i], q_norm[i], k_norm[i], conv_short[i], pool_w[i],
                                pool_scale[i], ssd_conv_w[i], ssd_conv_b[i], ssd_dt_bias[i],
                                ssd_a_log[i], ssd_d[i], ssd_norm[i], w_branch[i], w_out[i],
                                with_ctx_out)
        x = x + g1 * y
        h = modulate(rms_norm(x, norm_ffn[i]), sh2, sc2)
        x = x + g2 * conv_ffn(h, w_up[i], ffn_conv[i], w_down[i])
        if with_ctx_out:
            xc = xc + cg1 * yc
            hc = modulate(rms_norm(xc, norm_ffn[i]), csh2, csc2)
            xc = xc + cg2 * conv_ffn(hc, w_up[i], ffn_conv[i], w_down[i])
    return rms_norm(x, final_norm)
```

```python
from concourse.bass_utils import run_bass_kernel_spmd
import numpy as np
import concourse.bass as bass
import concourse.mybir as mybir
from contextlib import ExitStack

F32 = mybir.dt.float32
BF16 = mybir.dt.bfloat16
AF = mybir.ActivationFunctionType
ALU = mybir.AluOpType
AX = mybir.AxisListType


import types


def _freeze(fn):
    if fn is None or fn.__closure__ is None:
        return fn
    cells = tuple(types.CellType(c.cell_contents) for c in fn.__closure__)
    return types.FunctionType(fn.__code__, fn.__globals__, fn.__name__, fn.__defaults__, cells)


class Reg:
    __slots__ = ("name", "w", "rd")

    def __init__(self, name=""):
        self.name = name
        self.w = None
        self.rd = {}


class Op:
    __slots__ = ("en", "fn", "waits", "needed", "seq", "val", "dma", "dsem", "dval")

    def __init__(self, en, fn):
        self.en, self.fn = en, fn
        self.waits = []
        self.needed = False
        self.seq = 0
        self.val = None
        self.dma = False
        self.dsem = None
        self.dval = 0


class KB:
    ENGS = ("pe", "act", "dve", "pool", "sp")

    def __init__(self, nc, es, n_dma_sems=24):
        self.nc, self.es = nc, es
        self.eng = {"pe": nc.tensor, "act": nc.scalar, "dve": nc.vector, "pool": nc.gpsimd, "sp": nc.sync}
        self.sem = {e: es.enter_context(nc.semaphore("s_" + e)) for e in self.ENGS}
        self.ops = []
        self.nseq = {e: 0 for e in self.ENGS}
        self.seen = {e: {} for e in self.ENGS}
        self.seen_d = {e: {} for e in self.ENGS}
        self.dsems = [es.enter_context(nc.semaphore("d%d" % i)) for i in range(n_dma_sems)]
        self.dlast = [None] * n_dma_sems
        self.dval = [0] * n_dma_sems
        self.drr = 0
        self.drr2 = 0
        self.last = {}
        self.mute = False

    @staticmethod
    def is_dram(ap):
        return "DRAM" in str(ap.space).upper() or "DRAM" in type(ap.tensor).__name__.upper()

    def _need(self, op, dep):
        if dep is None:
            return
        E = op.en
        if dep.dma:
            i = dep.dsem
            if self.seen_d[E].get(i, 0) >= dep.dval:
                return
            self.seen_d[E][i] = dep.dval
            op.waits.append(("d", i, dep.dval))
        else:
            if dep.en == "pe" and E == "pe":
                return
            if self.seen[E].get(dep.en, 0) >= dep.seq:
                return
            self.seen[E][dep.en] = dep.seq
            dep.needed = True
            op.waits.append(("e", dep))

    def _deps(self, op, r, w):
        for reg in r:
            self._need(op, reg.w)
        for reg in w:
            self._need(op, reg.w)
            for rd in reg.rd.values():
                self._need(op, rd)
        for reg in r:
            reg.rd[op.en if not op.dma else ("d", id(op))] = op
        for reg in w:
            reg.w = op
            reg.rd = {}

    def op(self, en, fn, r=(), w=()):
        if self.mute:
            return None
        o = Op(en, _freeze(fn))
        self.nseq[en] += 1
        o.seq = self.nseq[en]
        self._deps(o, r, w)
        self.ops.append(o)
        self.last[en] = o
        return o

    def dma(self, out, in_, r=(), w=(), q=None, **kw):
        if self.mute:
            return None
        if q is None:
            q = "pool" if self.is_dram(out) else "sp"
        o = Op(q, lambda e: e.dma_start(out=out, in_=in_, **kw))
        o.dma = True
        self.nseq[q] += 1
        o.seq = self.nseq[q]
        nsp = (2 * len(self.dsems)) // 3
        if q == "sp":
            i = self.drr % nsp
            self.drr += 1
        else:
            i = nsp + self.drr2 % (len(self.dsems) - nsp)
            self.drr2 += 1
        prev = self.dlast[i]
        if prev is not None:
            self._need(o, prev)
        self._deps(o, r, w)
        self.dval[i] += 16
        o.dsem, o.dval = i, self.dval[i]
        self.dlast[i] = o
        self.ops.append(o)
        return o

    def full_barrier(self):
        lasts = [self.last.get(e) for e in self.ENGS]
        dl = [d for d in self.dlast if d is not None]
        for en in self.ENGS:
            o = Op(en, None)
            self.nseq[en] += 1
            o.seq = self.nseq[en]
            for d in lasts:
                if d is not None:
                    self._need(o, d)
            for d in dl:
                self._need(o, d)
            self.ops.append(o)

    def barrier(self, regs):
        o = Op("sp", None)
        self.nseq["sp"] += 1
        o.seq = self.nseq["sp"]
        for reg in regs:
            self._need(o, reg.w)
        self.ops.append(o)

    def emit(self):
        cnt = {e: 0 for e in self.ENGS}
        for o in self.ops:
            e = self.eng[o.en]
            for wt in o.waits:
                if wt[0] == "d":
                    e.wait_ge(self.dsems[wt[1]], wt[2])
                else:
                    d = wt[1]
                    assert d.val is not None, "dep emitted after consumer"
                    e.wait_ge(self.sem[d.en], d.val)
            if o.fn is None:
                continue
            ins = o.fn(e)
            if o.dma:
                ins.then_inc(self.dsems[o.dsem], 16)
            elif o.needed:
                cnt[o.en] += 1
                o.val = cnt[o.en]
                ins.then_inc(self.sem[o.en], 1)
        return cnt


import math


class Cfg:
    def __init__(self, D=1024, SEQ=8192, CTX=256, DFF=2816, GRID_W=64, DEPTH=2, SBW=2048):
        self.D, self.SEQ, self.CTX, self.DFF, self.GRID_W, self.DEPTH = D, SEQ, CTX, DFF, GRID_W, DEPTH
        self.DC = D // 128
        self.FC = DFF // 128
        self.T = SEQ
        self.NT = CTX + SEQ
        self.PAD = 8
        self.NE = self.NT + 4 * self.PAD
        self.CT = CTX // 128
        self.LT = SEQ // 128
        self.NTILE = self.CT + self.LT
        self.IN = 4112 + 4 * D
        self.EPS = 1e-6
        self.SBW = SBW
        self.stages = None
        self.sbl = 9
        self.sgl = 9

    def ext(self, col):
        return col + self.PAD if col < self.CTX else col + 3 * self.PAD

    def segs(self):
        return [(1, 0, self.CTX), (0, self.CTX, self.T)]


OFF = dict(q=0, k=512, v=640, bg=768, cg=1280, u=1792, pu=2304, z=2816, xbc=3328, dt=4096, gl=4112)


def build_program(cfg):
    nc = bass.Bass("TRN2", target_bir_lowering=False)
    C = cfg
    D, DC, NT, NE, T, CTX, FC, DFF = C.D, C.DC, C.NT, C.NE, C.T, C.CTX, C.FC, C.DFF
    L = C.DEPTH
    es = ExitStack()
    kb = KB(nc, es)

    def din(name, shape, dt=F32):
        return nc.dram_tensor(name, list(shape), dt, kind="ExternalInput").ap()

    def dscr(name, shape, dt):
        return nc.dram_tensor(name, list(shape), dt, kind="Internal").ap()

    x_in = din("x", [T, D])
    ctx_in = din("ctx", [CTX, D])
    cond_in = din("cond", [128, DC, 2])
    w_mod = din("w_mod", [L, D, 6 * D])
    w_in = din("w_in", [L, D, C.IN])
    pool_w = din("pool_w", [L, 4, 128, 128])
    w_branch = din("w_branch", [L, 4, 512, D])
    w_out = din("w_out", [L, D, D])
    w_up = din("w_up", [L, D, 2 * DFF])
    w_down = din("w_down", [L, DFF, D])
    NPF = 6 * DC + 2 * DC + 12 + 4 + 18 + 6 + 6 * FC
    pf_in = din("pf", [L, 128, NPF])
    NPT = 512 + 128 + 512 + 16 + 16 + 8
    pt_in = din("pt", [L, 128, NPT])
    fnorm_in = din("fnorm", [128, D])
    rope_in = din("rope", [128, C.LT, 64])
    invc_in = din("invc", [128, 4, 2, 16])
    cst_in = din("cst", [128, 7, 128])
    out = nc.dram_tensor("out", [T, D], F32, kind="ExternalOutput").ap()

    xd = dscr("xd", [128, DC, NT], F32)
    hd = dscr("hd", [128, DC, NE], BF16)
    qd = dscr("qd", [128, 4, NT], BF16)
    attd = dscr("attd", [64, 8, NT], BF16)
    bcd = dscr("bcd", [128, 4, NT], BF16)
    bpd = dscr("bpd", [128, 4, NT], BF16)
    bsd = dscr("bsd", [128, 4, NT], BF16)
    md = dscr("md", [128, DC, NT], BF16)
    xsd = dscr("xsd", [128, C.NTILE, 512], BF16)
    zsd = dscr("zsd", [128, C.NTILE, 512], BF16)
    bmd = dscr("bmd", [128, C.NTILE, 128], BF16)
    bcT = dscr("bcT", [128, 2, NT], BF16)
    dtd = dscr("dtd", [128, C.NTILE, 32], F32)
    yd = dscr("yd", [128, C.NTILE, 512], F32)
    R_xd, R_hd, R_qd, R_attd, R_bcd, R_bpd, R_bsd, R_md = [Reg(n) for n in "xd hd qd attd bcd bpd bsd md".split()]
    R_ssd = Reg("ssdprep")
    R_yd = Reg("yd")
    R_out = Reg("out")

    _uc = [0]

    def U(name):
        _uc[0] += 1
        return "t%d_%s" % (_uc[0], name)

    def sb(name, shape, dt):
        return es.enter_context(nc.sbuf_tensor(U(name), list(shape), dt))

    psb = [es.enter_context(nc.psum_tensor("ps%d" % i, [128, 512], F32)) for i in range(8)]
    R_ps = [Reg("ps%d" % i) for i in range(8)]
    pctr = [0]

    def psum():
        i = pctr[0] % 8
        pctr[0] += 1
        return psb[i], R_ps[i]

    class PsRot:
        def __init__(self, idx):
            self.idx, self.i = list(idx), 0

        def get(self):
            k = self.idx[self.i % len(self.idx)]
            self.i += 1
            return psb[k], R_ps[k]

    cst = sb("cst", [128, 7, 128], F32)
    R_cst = Reg("cst")
    kb.dma(cst[:], cst_in[:, :, :], w=[R_cst])
    identb = sb("identb", [128, 128], BF16)
    onesb = sb("onesb", [128, 128], BF16)
    zerob = sb("zerob", [128, DC, 16], BF16)
    kb.op("dve", lambda e: e.tensor_copy(identb[:], cst[:, 0, :]), r=[R_cst], w=[R_cst])
    kb.op("dve", lambda e: e.memset(onesb[:], 1.0), w=[R_cst])
    kb.op("dve", lambda e: e.memset(zerob[:], 0.0), w=[R_cst])
    identf = cst[:, 0, :]
    onesf = sb("onesf", [128, 128], F32)
    kb.op("pool", lambda e: e.memset(onesf[:], 1.0), w=[R_cst])
    cond = sb("cond", [128, DC, 2], F32)
    conds = sb("conds", [128, DC, 2], F32)
    R_cond = Reg("cond")
    kb.dma(cond[:], cond_in[:, :, :], w=[R_cond])
    kb.op("act", lambda e: e.activation(out=conds[:], in_=cond[:], func=AF.Silu), r=[R_cond], w=[R_cond])
    for c0 in (0, C.PAD + CTX, 2 * C.PAD + CTX, 3 * C.PAD + NT):
        kb.dma(hd[:, :, c0:c0 + C.PAD], zerob[:, :, 0:C.PAD], r=[R_cst], w=[R_hd])

    pf = sb("pf", [128, NPF], F32)
    pt = sb("pt", [128, NPT], F32)
    R_pf = Reg("pf")
    modT = sb("modT", [128, 6 * DC, 2], F32)
    AB = sb("AB", [128, 6, DC, 2], F32)
    R_mod = Reg("mod")
    R_rope = Reg("rope")
    invc = sb("invc", [128, 4, 2, 16], F32)
    kb.dma(invc[:], invc_in[:, :, :, :], w=[R_rope])

    def rsqrt(out_ap, in_ap, const, r, w):
        kb.op("act", lambda e: e.activation(out=out_ap, in_=in_ap, func=AF.Sqrt, bias=float(const)), r=r, w=w)
        kb.op("dve", lambda e: e.reciprocal(out_ap, out_ap), r=w, w=w)

    def stage_scope():
        kb.full_barrier()
        return ExitStack()

    def mark(name):
        kb.mute = (C.stages is not None) and (name not in C.stages)

    class Rot:
        def __init__(self, st, name, shape, dt, n=2):
            self.t = [st.enter_context(nc.sbuf_tensor(U("%s%d" % (name, i)), list(shape), dt)) for i in range(n)]
            self.r = [Reg("%s%d" % (name, i)) for i in range(n)]
            self.i = 0

        def get(self):
            k = self.i % len(self.t)
            self.i += 1
            return self.t[k], self.r[k]

    def load_w(st, name, src_ap, shape, rot_stage=None, eng="pool"):
        wt = st.enter_context(nc.sbuf_tensor(U(name), list(shape), BF16))
        rw = Reg(name)
        n1 = shape[-1]
        P = shape[0]
        stg = rot_stage
        for c in range(shape[1]) if len(shape) == 3 else [None]:
            s_t, s_r = stg.get()
            if c is None:
                kb.dma(s_t[0:P, 0:n1], src_ap, w=[s_r])
                kb.op(eng, lambda e, s_t=s_t: e.tensor_copy(wt[:, :], s_t[0:P, 0:n1]), r=[s_r], w=[rw])
            else:
                kb.dma(s_t[0:P, 0:n1], src_ap[:, c, :], w=[s_r])
                kb.op(eng, lambda e, s_t=s_t, c=c: e.tensor_copy(wt[:, c, :], s_t[0:P, 0:n1]), r=[s_r], w=[rw])
        return wt, rw

    def wview(w, li, c0, n):
        return w[li, :, c0:c0 + n].rearrange("(c p) n -> p c n", p=128)

    mark("PRO")
    with stage_scope() as st:
        xin = Rot(st, "xin", [128, D], F32)
        xo = Rot(st, "xo", [128, DC, 128], F32)
        for ti in range(C.NTILE):
            src = ctx_in[ti * 128:(ti + 1) * 128, :] if ti < C.CT else x_in[(ti - C.CT) * 128:(ti - C.CT + 1) * 128, :]
            xt, xr = xin.get()
            kb.dma(xt[:], src, w=[xr])
            ot, orr = xo.get()
            for c4 in range(0, DC, 4):
                pst, psr = psum()
                for c in range(c4, min(c4 + 4, DC)):
                    kb.op("pe", lambda e, pst=pst, xt=xt, c=c, c4=c4: e.transpose(pst[:, (c - c4) * 128:(c - c4 + 1) * 128], xt[:, c * 128:(c + 1) * 128], identf), r=[xr, R_cst], w=[psr])
                nn = min(4, DC - c4)
                kb.op("act", lambda e, pst=pst, ot=ot, c4=c4, nn=nn: e.activation(out=ot[:, c4:c4 + nn, :], in_=pst[:, 0:nn * 128].rearrange("p (c t) -> p c t", t=128), func=AF.Copy), r=[psr], w=[orr])
            kb.dma(xd[:, :, ti * 128:(ti + 1) * 128], ot[:], r=[orr], w=[R_xd])

    def norm_block(st_rot, xt, xr, c0, n, seg, ia, ib):
        sq, sqr = st_rot["sq"].get()
        kb.op("act", lambda e: e.activation(out=sq[:, :, 0:n], in_=xt[:, :, 0:n], func=AF.Square), r=[xr], w=[sqr])
        pst, psr = psum()
        for c in range(DC):
            kb.op("pe", lambda e, c=c: e.matmul(pst[:, 0:n], onesb[:], sq[:, c, 0:n], start=(c == 0), stop=(c == DC - 1)), r=[sqr, R_cst], w=[psr])
        rs, rsr = st_rot["rs"].get()
        rsqrt(rs[:, 0:n], pst[:, 0:n], D * C.EPS, [psr], [rsr])
        tm, tmr = st_rot["tm"].get()
        kb.op("dve", lambda e: e.tensor_tensor(tm[:, :, 0:n], xt[:, :, 0:n], rs[:, 0:n].unsqueeze(1).broadcast_to([128, DC, n]), ALU.mult), r=[xr, rsr], w=[tmr])
        hb, hbr = st_rot["hb"].get()
        for c in range(DC):
            kb.op("pool" if c % 2 else "dve", lambda e, c=c: e.tensor_scalar(hb[:, c, 0:n], tm[:, c, 0:n], AB[:, ia, c, seg:seg + 1], AB[:, ib, c, seg:seg + 1], ALU.mult, ALU.add), r=[tmr, R_mod], w=[hbr])
        kb.dma(hd[:, :, C.ext(c0):C.ext(c0) + n], hb[:, :, 0:n], r=[hbr], w=[R_hd])

    def blocks(w):
        res = []
        for seg, s0, ln in C.segs():
            c = 0
            while c < ln:
                n = min(w, ln - c)
                res.append((seg, s0 + c, n, c == 0, c + n == ln))
                c += n
        return res

    for li in range(L):
        last = li == L - 1
        segs_out = [(0, CTX, T)] if last else C.segs()

        mark("MOD")
        with stage_scope() as st:
            kb.dma(pf[:], pf_in[li, :, :], w=[R_pf])
            kb.dma(pt[:], pt_in[li, :, :], w=[R_pf])
            wst = Rot(st, "wmst", [128, DC, 512], F32)
            pst, psr = psum()
            for cb in range(0, 6 * D, 512):
                wt, wr = wst.get()
                kb.dma(wt[:], w_mod[li, :, cb:cb + 512].rearrange("(c p) n -> p c n", p=128), w=[wr])
                for j in range(4):
                    cc = cb // 128 + j
                    for c in range(DC):
                        kb.op("pe", lambda e, wt=wt, j=j, c=c, cc=cc: e.matmul(pst[:, cc * 2:cc * 2 + 2], wt[:, c, j * 128:(j + 1) * 128], conds[:, c, :], start=(c == 0), stop=(c == DC - 1)), r=[wr, R_cond], w=[psr])
            bm_ = pf[:, 0:6 * DC]
            kb.op("dve", lambda e: e.tensor_tensor(modT[:], pst[:, 0:12 * DC].rearrange("p (c j) -> p c j", j=2), bm_.unsqueeze(2).broadcast_to([128, 6 * DC, 2]), ALU.add), r=[psr, R_pf], w=[R_mod])
            mv = lambda k: modT[:, k * DC:(k + 1) * DC, :]
            nmix = pf[:, 6 * DC:7 * DC].unsqueeze(2).broadcast_to([128, DC, 2])
            nffn = pf[:, 7 * DC:8 * DC].unsqueeze(2).broadcast_to([128, DC, 2])
            sD = float(math.sqrt(D))
            for (ia, ksc, ksh, kg, nrm) in ((0, 1, 0, 2, nmix), (3, 4, 3, 5, nffn)):
                kb.op("dve", lambda e, ia=ia, ksc=ksc: e.tensor_scalar(AB[:, ia], mv(ksc), 1.0, sD, ALU.add, ALU.mult), r=[R_mod], w=[R_mod])
                kb.op("dve", lambda e, ia=ia, nrm=nrm: e.tensor_tensor(AB[:, ia], AB[:, ia], nrm, ALU.mult), r=[R_mod, R_pf], w=[R_mod])
                kb.op("dve", lambda e, ia=ia, ksh=ksh: e.tensor_copy(AB[:, ia + 1], mv(ksh)), r=[R_mod], w=[R_mod])
                kb.op("dve", lambda e, ia=ia, kg=kg: e.tensor_copy(AB[:, ia + 2], mv(kg)), r=[R_mod], w=[R_mod])
            kb.op("act", lambda e: e.activation(out=pt[:, 1168:1184], in_=pt[:, 1168:1184], func=AF.Exp), r=[R_pf], w=[R_pf])
            kb.op("dve", lambda e: e.tensor_scalar(pt[:, 1168:1184], pt[:, 1168:1184], -1.0, None, ALU.mult), r=[R_pf], w=[R_pf])
            kb.op("dve", lambda e: e.tensor_scalar(pt[:, 640:1152], pt[:, 640:1152], float(math.sqrt(512.0)), None, ALU.mult), r=[R_pf], w=[R_pf])

        P_CS = 8 * DC
        P_PS = P_CS + 12
        P_SW = P_PS + 4
        P_SB = P_SW + 18
        P_FC = P_SB + 6
        qg, kg_, g512, dtb, arate, dsk = pt[:, 0:512], pt[:, 512:640], pt[:, 640:1152], pt[:, 1152:1168], pt[:, 1168:1184], pt[:, 1184:1192]

        mark("SA")
        def norm_rots(st, W):
            return dict(sq=Rot(st, "sq", [128, DC, W], BF16, 1), rs=Rot(st, "rs", [128, W], F32, 1),
                        tm=Rot(st, "tm", [128, DC, W], F32, 1), hb=Rot(st, "hb", [128, DC, W], BF16, 2))

        with stage_scope() as st:
            rots = norm_rots(st, 512)
            xb = Rot(st, "xb", [128, DC, 512], F32)
            for seg, c0, n, f, l_ in blocks(512):
                xt, xr = xb.get()
                kb.dma(xt[:, :, 0:n], xd[:, :, c0:c0 + n], r=[R_xd], w=[xr])
                norm_block(rots, xt, xr, c0, n, seg, 0, 1)

        mark("SB")
        stB = stage_scope()
        KT = stB.enter_context(nc.sbuf_tensor(U("KT"), [128, NT], BF16))
        VF = stB.enter_context(nc.sbuf_tensor(U("VF"), [128, C.NTILE, 2, 65], BF16))
        R_KT, R_VF = Reg("KT"), Reg("VF")
        kb.op("dve", lambda e: e.memset(VF[:].rearrange("p a b c -> p (a b c)"), 1.0), w=[R_VF])
        rope = stB.enter_context(nc.sbuf_tensor(U("rope"), [128, C.LT, 64], F32))
        kb.dma(rope[:], rope_in[:, :, :], w=[R_rope])
        with ExitStack() as st:
            stg = Rot(st, "wstg", [128, 768], F32)
            wq, wqr = load_w(st, "wqkv", wview(w_in, li, 0, 768), [128, DC, 768], stg)
            hb = Rot(st, "hbq", [128, DC, 128], BF16)
            t32 = Rot(st, "t32", [128, 640], F32, 2)
            t32b = Rot(st, "t32b", [128, 640], F32, 2)
            ssr = Rot(st, "ssr", [128, 16], F32, 2)
            qkb = Rot(st, "qkb", [128, 640], BF16, 2)
            qk2 = Rot(st, "qk2", [128, 512], BF16, 2)
            qT = Rot(st, "qT", [128, 4, 128], BF16, 2)
            for ti in range(C.NTILE):
                isctx = ti < C.CT
                c0 = ti * 128
                ht, hr = hb.get()
                kb.dma(ht[:], hd[:, :, C.ext(c0):C.ext(c0) + 128], r=[R_hd], w=[hr])
                pq, pqr = psum()
                pk, pkr = psum()
                for c in range(DC):
                    kb.op("pe", lambda e, c=c, ht=ht, pq=pq: e.matmul(pq[:, 0:512], ht[:, c, :], wq[:, c, 0:512], start=(c == 0), stop=(c == DC - 1)), r=[hr, wqr], w=[pqr])
                for c in range(DC):
                    kb.op("pe", lambda e, c=c, ht=ht, pk=pk: e.matmul(pk[:, 0:256], ht[:, c, :], wq[:, c, 512:768], start=(c == 0), stop=(c == DC - 1)), r=[hr, wqr], w=[pkr])
                kb.op("act", lambda e, pk=pk, ti=ti: e.activation(out=VF[:, ti, :, 0:64], in_=pk[:, 128:256].rearrange("p (g d) -> p g d", d=64), func=AF.Copy), r=[pkr], w=[R_VF])
                need_q = (not isctx) or (not last)
                if C.sbl < 2:
                    continue
                a, ar = t32.get()
                b, br = t32b.get()
                s, sr = ssr.get()
                qb, qbr = qkb.get()
                if need_q:
                    kb.op("act", lambda e, a=a, pq=pq: e.activation(out=a[:, 0:512], in_=pq[:, 0:512], func=AF.Copy), r=[pqr], w=[ar])
                kb.op("act", lambda e, a=a, pk=pk: e.activation(out=a[:, 512:640], in_=pk[:, 0:128], func=AF.Copy), r=[pkr], w=[ar])
                lo = 0 if need_q else 512
                nh = (640 - lo) // 64
                kb.op("dve", lambda e, a=a, b=b, lo=lo: e.tensor_tensor(b[:, lo:640], a[:, lo:640], a[:, lo:640], ALU.mult), r=[ar], w=[br])
                kb.op("dve", lambda e, s=s, b=b, lo=lo, nh=nh: e.tensor_reduce(s[:, 0:nh], b[:, lo:640].rearrange("p (h d) -> p h d", d=64), AX.X, ALU.add), r=[br], w=[sr])
                rsqrt(s[:, 0:nh], s[:, 0:nh], 64 * C.EPS, [sr], [sr])
                kb.op("dve", lambda e, a=a, s=s, lo=lo, nh=nh: e.tensor_tensor(a[:, lo:640].rearrange("p (h d) -> p h d", d=64), a[:, lo:640].rearrange("p (h d) -> p h d", d=64), s[:, 0:nh].unsqueeze(2).broadcast_to([128, nh, 64]), ALU.mult), r=[ar, sr], w=[ar])
                kb.op("dve", lambda e, a=a, lo=lo: e.tensor_tensor(a[:, lo:640], a[:, lo:640], pt[:, lo:640], ALU.mult), r=[ar, R_pf], w=[ar])
                if C.sbl < 3:
                    continue
                if isctx:
                    kb.op("dve", lambda e, a=a, qb=qb, lo=lo: e.tensor_copy(qb[:, lo:640], a[:, lo:640]), r=[ar], w=[qbr])
                else:
                    li_ = ti - C.CT
                    av = a[:, 0:640].rearrange("p (h x q f) -> p h x q f", x=2, q=2, f=16)
                    bv = b[:, 0:640].rearrange("p (h x q f) -> p h x q f", x=2, q=2, f=16)
                    qv = qb[:, 0:640].rearrange("p (h x q f) -> p h x q f", x=2, q=2, f=16)
                    for x in range(2):
                        cs = rope[:, li_, x * 16:(x + 1) * 16].unsqueeze(1).broadcast_to([128, 10, 16])
                        sn = rope[:, li_, 32 + x * 16:32 + (x + 1) * 16].unsqueeze(1).broadcast_to([128, 10, 16])
                        t1, t2 = av[:, :, x, 0, :], av[:, :, x, 1, :]
                        b1, b2 = bv[:, :, x, 0, :], bv[:, :, x, 1, :]
                        kb.op("dve", lambda e, b1=b1, t1=t1, cs=cs: e.tensor_tensor(b1, t1, cs, ALU.mult), r=[ar, R_rope], w=[br])
                        kb.op("dve", lambda e, b2=b2, t2=t2, sn=sn: e.tensor_tensor(b2, t2, sn, ALU.mult), r=[ar, R_rope], w=[br])
                        kb.op("dve", lambda e, b1=b1, b2=b2, qv=qv, x=x: e.tensor_tensor(qv[:, :, x, 0, :], b1, b2, ALU.subtract), r=[br], w=[qbr])
                        kb.op("dve", lambda e, b1=b1, t1=t1, sn=sn: e.tensor_tensor(b1, t1, sn, ALU.mult), r=[ar, R_rope, qbr], w=[br])
                        kb.op("dve", lambda e, b2=b2, t2=t2, cs=cs: e.tensor_tensor(b2, t2, cs, ALU.mult), r=[ar, R_rope, qbr], w=[br])
                        kb.op("dve", lambda e, b1=b1, b2=b2, qv=qv, x=x: e.tensor_tensor(qv[:, :, x, 1, :], b1, b2, ALU.add), r=[br], w=[qbr])
                if C.sbl < 4:
                    continue
                ptp, ptr = psum()
                pb = ptp[:, :].bitcast(BF16)
                kb.op("pe", lambda e, pb=pb, qb=qb: e.transpose(pb[:, 0:128], qb[:, 512:640], identb[:]), r=[qbr, R_cst], w=[ptr])
                if need_q and C.sbl >= 5:
                    q2, q2r = qk2.get()
                    for g in range(2):
                        kb.op("dve", lambda e, q2=q2, qb=qb, g=g: e.tensor_copy(q2[:].rearrange("p (r g d) -> p r g d", g=2, r=4)[:, :, g, :], qb[:, g * 256:(g + 1) * 256].rearrange("p (r d) -> p r d", d=64)), r=[qbr], w=[q2r])
                    for j in range(4):
                        kb.op("pe", lambda e, pb=pb, j=j, q2=q2: e.transpose(pb[:, 128 * (j + 1):128 * (j + 2)], q2[:, j * 128:(j + 1) * 128], identb[:]), r=[q2r, R_cst], w=[ptr])
                kb.op("act", lambda e, pb=pb, c0=c0: e.activation(out=KT[:, c0:c0 + 128], in_=pb[:, 0:128], func=AF.Copy), r=[ptr], w=[R_KT])
                if need_q and C.sbl >= 6:
                    qt_, qtr = qT.get()
                    kb.op("act", lambda e, pb=pb, qt_=qt_: e.activation(out=qt_[:], in_=pb[:, 128:640].rearrange("p (j t) -> p j t", t=128), func=AF.Copy), r=[ptr], w=[qtr])
                    kb.dma(qd[:, :, c0:c0 + 128], qt_[:], r=[qtr], w=[R_qd])

        mark("SC")
        with ExitStack() as st:
            kb.full_barrier()
            qbk = Rot(st, "qbk", [128, 4, 512], BF16)
            pT = Rot(st, "pT", [128, 512], BF16, 4)
            otm = Rot(st, "otm", [65, 512], F32, 2)
            rcp = Rot(st, "rcp", [65, 512], F32, 2)
            oTn = Rot(st, "oTn", [64, 512], BF16, 2)
            ones65 = st.enter_context(nc.sbuf_tensor(U("ones65"), [65, 64], F32))
            kb.op("pool", lambda e: e.memset(ones65[:], 1.0), w=[R_cst])
            psO, psS, psB = PsRot([0, 1]), PsRot([2, 3, 4, 5]), PsRot([6, 7])
            qblocks = []
            if not last:
                qblocks.append((0, CTX, list(range(C.CT))))
            for c in range(CTX, NT, 512):
                qblocks.append((c, min(512, NT - c), list(range(C.NTILE))))
            for (c0, n, ktiles) in qblocks:
                qt_, qtr = qbk.get()
                kb.dma(qt_[:, :, 0:n], qd[:, :, c0:c0 + n], r=[R_qd], w=[qtr])
                for j in range(4):
                    po = [psO.get(), psO.get()]
                    for ki, kt in enumerate(ktiles):
                        for g in range(2):
                            pS, pSr = psS.get()
                            rows = slice(g * 64, g * 64 + 64)
                            kb.op("pe", lambda e, pS=pS, rows=rows, kt=kt, qt_=qt_, j=j: e.matmul(pS[:, 0:n], KT[rows, kt * 128:(kt + 1) * 128], qt_[rows, j, 0:n], start=True, stop=True), r=[R_KT, qtr], w=[pSr])
                            p_, pr_ = pT.get()
                            kb.op("act", lambda e, p_=p_, pS=pS: e.activation(out=p_[:, 0:n], in_=pS[:, 0:n], func=AF.Exp, scale=8.0), r=[pSr], w=[pr_])
                            kb.op("pe", lambda e, g=g, kt=kt, p_=p_, po=po, ki=ki: e.matmul(po[g][0][0:65, 0:n], VF[:, kt, g, :], p_[:, 0:n], start=(ki == 0), stop=(ki == len(ktiles) - 1)), r=[R_VF, pr_], w=[po[g][1]])
                    for g in range(2):
                        h = j + 4 * g
                        o_, or_ = otm.get()
                        kb.op("act", lambda e, o_=o_, g=g, po=po: e.activation(out=o_[:, 0:n], in_=po[g][0][0:65, 0:n], func=AF.Copy), r=[po[g][1]], w=[or_])
                        r_, rr_ = rcp.get()
                        kb.op("dve", lambda e, o_=o_: e.reciprocal(o_[64:65, 0:n], o_[64:65, 0:n]), r=[or_], w=[or_])
                        kb.op("dve", lambda e, o_=o_, r_=r_: e.tensor_copy(r_[64:65, 0:n], o_[64:65, 0:n]), r=[or_], w=[rr_])
                        pB, pBr = psB.get()
                        kb.op("pe", lambda e, pB=pB, r_=r_: e.matmul(pB[0:64, 0:n], ones65[64:65, :], r_[64:65, 0:n], start=True, stop=True), r=[rr_, R_cst], w=[pBr])
                        on_, onr = oTn.get()
                        kb.op("dve", lambda e, on_=on_, o_=o_, pB=pB: e.tensor_tensor(on_[:, 0:n], o_[0:64, 0:n], pB[0:64, 0:n], ALU.mult), r=[or_, pBr], w=[onr])
                        kb.dma(attd[:, h, c0:c0 + n], on_[:, 0:n], r=[onr], w=[R_attd])
        stB.close()

        mark("SD")
        with stage_scope() as st:
            stg = Rot(st, "wstg", [128, 1536], F32)
            wc, wcr = load_w(st, "wconv", wview(w_in, li, OFF["bg"], 1536), [128, DC, 1536], stg)
            hbk = Rot(st, "hbk", [128, DC, 512], BF16)
            cgs = Rot(st, "cgs", [128, 512], F32, 2)
            cu = Rot(st, "cu", [128, 512], F32, 2)
            tt = Rot(st, "tt", [128, 512], F32, 2)
            ob = Rot(st, "ob", [128, 4, 512], BF16, 2)
            for seg, c0, n, f, l_ in blocks(510):
                if last and seg == 1:
                    continue
                ht, hr = hbk.get()
                kb.dma(ht[:, :, 0:n + 2], hd[:, :, C.ext(c0) - 1:C.ext(c0) + n + 1], r=[R_hd], w=[hr])
                o_, or_ = ob.get()
                for i in range(4):
                    pss = [psum() for _ in range(3)]
                    for k, nm in enumerate(("bg", "cg", "u")):
                        wc0 = (OFF[nm] - OFF["bg"]) + i * 128
                        for c in range(DC):
                            kb.op("pe", lambda e, k=k, c=c, wc0=wc0, ht=ht, pss=pss: e.matmul(pss[k][0][:, 0:n + 2], wc[:, c, wc0:wc0 + 128], ht[:, c, 0:n + 2], start=(c == 0), stop=(c == DC - 1)), r=[hr, wcr], w=[pss[k][1]])
                    g_, gr_ = cgs.get()
                    kb.op("act", lambda e, g_=g_, pss=pss: e.activation(out=g_[:, 0:n + 2], in_=pss[1][0][:, 0:n + 2], func=AF.Copy), r=[pss[1][1]], w=[gr_])
                    u_, ur_ = cu.get()
                    kb.op("dve", lambda e, u_=u_, g_=g_, pss=pss: e.tensor_tensor(u_[:, 0:n + 2], g_[:, 0:n + 2], pss[2][0][:, 0:n + 2], ALU.mult), r=[gr_, pss[2][1]], w=[ur_])
                    t_, tr_ = tt.get()
                    w3 = pf[:, P_CS + 3 * i:P_CS + 3 * i + 3]
                    kb.op("pool", lambda e, t_=t_, u_=u_, w3=w3: e.tensor_scalar(t_[:, 0:n], u_[:, 1:n + 1], w3[:, 1:2], None, ALU.mult), r=[ur_, R_pf], w=[tr_])
                    kb.op("dve", lambda e, t_=t_, u_=u_, w3=w3: e.scalar_tensor_tensor(t_[:, 0:n], u_[:, 0:n], w3[:, 0:1], t_[:, 0:n], ALU.mult, ALU.add), r=[ur_, R_pf, tr_], w=[tr_])
                    kb.op("dve", lambda e, t_=t_, u_=u_, w3=w3: e.scalar_tensor_tensor(t_[:, 0:n], u_[:, 2:n + 2], w3[:, 2:3], t_[:, 0:n], ALU.mult, ALU.add), r=[ur_, R_pf, tr_], w=[tr_])
                    kb.op("dve", lambda e, t_=t_, o_=o_, i=i, pss=pss: e.tensor_tensor(o_[:, i, 0:n], t_[:, 0:n], pss[0][0][:, 1:n + 1], ALU.mult), r=[tr_, pss[0][1]], w=[or_])
                kb.dma(bcd[:, :, c0:c0 + n], o_[:, :, 0:n], r=[or_], w=[R_bcd])

        mark("SE")
        with stage_scope() as st:
            stg = Rot(st, "wstg", [128, 512], F32)
            wp, wpr = load_w(st, "wpu", wview(w_in, li, OFF["pu"], 512), [128, DC, 512], stg)
            wpl, wplr = load_w(st, "wpool", pool_w[li].rearrange("g c d -> c g d"), [128, 4, 128], stg)
            hbk = Rot(st, "hbk", [128, DC, 512], BF16)
            u32 = Rot(st, "u32", [128, 512], F32, 2)
            sA = Rot(st, "sA", [128, 512], F32, 2)
            sB = Rot(st, "sB", [128, 512], F32, 2)
            pl = Rot(st, "pl", [128, 512], F32, 2)
            plb = Rot(st, "plb", [128, 512], BF16, 2)
            ob = Rot(st, "ob", [128, 4, 512], BF16, 2)
            for seg, c0, n, f, l_ in blocks(496):
                if last and seg == 1:
                    continue
                ht, hr = hbk.get()
                kb.dma(ht[:, :, 0:n + 16], hd[:, :, C.ext(c0) - 8:C.ext(c0) + n + 8], r=[R_hd], w=[hr])
                o_, or_ = ob.get()
                m = n + 16
                for g in range(4):
                    pu_, pur = psum()
                    for c in range(DC):
                        kb.op("pe", lambda e, c=c, g=g, ht=ht, pu_=pu_: e.matmul(pu_[:, 0:m], wp[:, c, g * 128:(g + 1) * 128], ht[:, c, 0:m], start=(c == 0), stop=(c == DC - 1)), r=[hr, wpr], w=[pur])
                    u_, ur_ = u32.get()
                    kb.op("act", lambda e, u_=u_, pu_=pu_: e.activation(out=u_[:, 0:m], in_=pu_[:, 0:m], func=AF.Copy), r=[pur], w=[ur_])
                    a_, ar_ = sA.get()
                    b_, br_ = sB.get()
                    kb.op("dve", lambda e, a_=a_, u_=u_: e.tensor_tensor(a_[:, 1:m], u_[:, 0:m - 1], u_[:, 1:m], ALU.add), r=[ur_], w=[ar_])
                    cur, curr, oth, othr = a_, ar_, b_, br_
                    lo, hi, sh = 1, m, 1
                    for step in range(g):
                        kb.op("dve", lambda e, cur=cur, oth=oth, lo=lo, hi=hi, sh=sh: e.tensor_tensor(oth[:, lo + sh:hi - sh], cur[:, lo:hi - 2 * sh], cur[:, lo + 2 * sh:hi], ALU.add), r=[curr], w=[othr])
                        lo, hi, sh = lo + sh, hi - sh, sh * 2
                        cur, curr, oth, othr = oth, othr, cur, curr
                    wnd = 2 ** (g + 1)
                    p_, pr_ = pl.get()
                    kb.op("dve", lambda e, p_=p_, cur=cur, u_=u_, wnd=wnd: e.scalar_tensor_tensor(p_[:, 0:n], cur[:, 8:8 + n], 1.0 / wnd, u_[:, 8:8 + n], ALU.mult, ALU.subtract), r=[curr, ur_], w=[pr_])
                    for (flag, cc, tc) in ((f, 0, 0), (l_, n - 8, 8)):
                        if flag:
                            kb.op("dve", lambda e, oth=oth, cur=cur, cc=cc, tc=tc, g=g, seg=seg: e.tensor_tensor(oth[:, 0:8], cur[:, 8 + cc:16 + cc], invc[:, g, seg, tc:tc + 8], ALU.mult), r=[curr, R_rope], w=[othr])
                            kb.op("dve", lambda e, oth=oth, p_=p_, u_=u_, cc=cc: e.tensor_tensor(p_[:, cc:cc + 8], oth[:, 0:8], u_[:, 8 + cc:16 + cc], ALU.subtract), r=[othr, ur_, pr_], w=[pr_])
                    pb_, pbr = plb.get()
                    kb.op("act", lambda e, pb_=pb_, p_=p_: e.activation(out=pb_[:, 0:n], in_=p_[:, 0:n], func=AF.Copy), r=[pr_], w=[pbr])
                    po_, por = psum()
                    kb.op("pe", lambda e, po_=po_, g=g, pb_=pb_: e.matmul(po_[:, 0:n], wpl[:, g, :], pb_[:, 0:n], start=True, stop=True), r=[wplr, pbr], w=[por])
                    kb.op("dve", lambda e, o_=o_, g=g, po_=po_: e.tensor_scalar(o_[:, g, 0:n], po_[:, 0:n], pf[:, P_PS + g:P_PS + g + 1], None, ALU.mult), r=[por, R_pf], w=[or_])
                kb.dma(bpd[:, :, c0:c0 + n], o_[:, :, 0:n], r=[or_], w=[R_bpd])

        mark("SF")
        with stage_scope() as st:
            stg = Rot(st, "wstg", [128, 1296], F32)
            ws, wsr = load_w(st, "wssd", wview(w_in, li, OFF["z"], 1296), [128, DC, 1296], stg)
            ZO, XO, DO = 0, 512, 1280
            hbk = Rot(st, "hbk", [128, DC, 386], BF16)
            cv = Rot(st, "cv", [128, 384], F32, 2)
            ac = Rot(st, "ac", [128, 384], BF16, 2)
            xsb = Rot(st, "xsb", [128, 3, 512], BF16, 2)
            bmb = Rot(st, "bmb", [128, 3, 128], BF16, 2)
            zsb = Rot(st, "zsb", [128, 3, 512], BF16, 2)
            dtb_ = Rot(st, "dtb", [128, 3, 32], F32, 2)
            d1 = Rot(st, "d1", [128, 16], F32, 2)
            d2 = Rot(st, "d2", [128, 16], F32, 2)
            for seg, c0, n, f, l_ in blocks(384):
                nt_ = n // 128
                t0 = c0 // 128
                ht, hr = hbk.get()
                kb.dma(ht[:, :, 0:n + 2], hd[:, :, C.ext(c0) - 1:C.ext(c0) + n + 1], r=[R_hd], w=[hr])
                xs_, xsr = xsb.get()
                bm_, bmr = bmb.get()
                for i in range(6):
                    px, pxr = psum()
                    for c in range(DC):
                        kb.op("pe", lambda e, c=c, i=i, px=px, ht=ht: e.matmul(px[:, 0:n + 2], ws[:, c, XO + i * 128:XO + (i + 1) * 128], ht[:, c, 0:n + 2], start=(c == 0), stop=(c == DC - 1)), r=[hr, wsr], w=[pxr])
                    t_, tr_ = cv.get()
                    w3 = pf[:, P_SW + 3 * i:P_SW + 3 * i + 3]
                    bb = pf[:, P_SB + i:P_SB + i + 1]
                    kb.op("dve", lambda e, t_=t_, px=px, w3=w3, bb=bb: e.tensor_scalar(t_[:, 0:n], px[:, 1:n + 1], w3[:, 1:2], bb, ALU.mult, ALU.add), r=[pxr, R_pf], w=[tr_])
                    kb.op("dve", lambda e, t_=t_, px=px, w3=w3: e.scalar_tensor_tensor(t_[:, 0:n], px[:, 0:n], w3[:, 0:1], t_[:, 0:n], ALU.mult, ALU.add), r=[pxr, R_pf, tr_], w=[tr_])
                    kb.op("dve", lambda e, t_=t_, px=px, w3=w3: e.scalar_tensor_tensor(t_[:, 0:n], px[:, 2:n + 2], w3[:, 2:3], t_[:, 0:n], ALU.mult, ALU.add), r=[pxr, R_pf, tr_], w=[tr_])
                    a_, ar_ = ac.get()
                    kb.op("act", lambda e, a_=a_, t_=t_: e.activation(out=a_[:, 0:n], in_=t_[:, 0:n], func=AF.Silu), r=[tr_], w=[ar_])
                    if i >= 4:
                        kb.dma(bcT[:, i - 4, c0:c0 + n], a_[:, 0:n], r=[ar_], w=[R_ssd])
                    if i <= 4:
                        ptp, ptr = psum()
                        pb = ptp[:, :].bitcast(BF16)
                        for k in range(nt_):
                            kb.op("pe", lambda e, pb=pb, k=k, a_=a_: e.transpose(pb[:, k * 128:(k + 1) * 128], a_[:, k * 128:(k + 1) * 128], identb[:]), r=[ar_, R_cst], w=[ptr])
                        if i < 4:
                            kb.op("act", lambda e, xs_=xs_, pb=pb, i=i: e.activation(out=xs_[:, 0:nt_, i * 128:(i + 1) * 128], in_=pb[:, 0:nt_ * 128].rearrange("p (k f) -> p k f", f=128), func=AF.Copy), r=[ptr], w=[xsr])
                        else:
                            kb.op("act", lambda e, bm_=bm_, pb=pb: e.activation(out=bm_[:, 0:nt_, :], in_=pb[:, 0:nt_ * 128].rearrange("p (k f) -> p k f", f=128), func=AF.Copy), r=[ptr], w=[bmr])
                kb.dma(xsd[:, t0:t0 + nt_, :], xs_[:, 0:nt_, :], r=[xsr], w=[R_ssd])
                kb.dma(bmd[:, t0:t0 + nt_, :], bm_[:, 0:nt_, :], r=[bmr], w=[R_ssd])
                z_, zr_ = zsb.get()
                dd, ddr = dtb_.get()
                for k in range(nt_):
                    pz, pzr = psum()
                    pd, pdr = psum()
                    for c in range(DC):
                        kb.op("pe", lambda e, c=c, k=k, pz=pz, ht=ht: e.matmul(pz[:, 0:512], ht[:, c, 1 + k * 128:1 + (k + 1) * 128], ws[:, c, ZO:ZO + 512], start=(c == 0), stop=(c == DC - 1)), r=[hr, wsr], w=[pzr])
                    for c in range(DC):
                        kb.op("pe", lambda e, c=c, k=k, pd=pd, ht=ht: e.matmul(pd[:, 0:16], ht[:, c, 1 + k * 128:1 + (k + 1) * 128], ws[:, c, DO:DO + 16], start=(c == 0), stop=(c == DC - 1)), r=[hr, wsr], w=[pdr])
                    kb.op("act", lambda e, z_=z_, k=k, pz=pz: e.activation(out=z_[:, k, :], in_=pz[:, 0:512], func=AF.Silu), r=[pzr], w=[zr_])
                    x1, x1r = d1.get()
                    x2, x2r = d2.get()
                    kb.op("dve", lambda e, x1=x1, pd=pd: e.tensor_tensor(x1[:], pd[:, 0:16], dtb, ALU.add), r=[pdr, R_pf], w=[x1r])
                    kb.op("act", lambda e, x1=x1, x2=x2: e.activation(out=x2[:], in_=x1[:], func=AF.Abs), r=[x1r], w=[x2r])
                    kb.op("act", lambda e, x2=x2: e.activation(out=x2[:], in_=x2[:], func=AF.Exp, scale=-1.0), r=[x2r], w=[x2r])
                    kb.op("act", lambda e, x2=x2: e.activation(out=x2[:], in_=x2[:], func=AF.Ln, bias=1.0), r=[x2r], w=[x2r])
                    kb.op("dve", lambda e, dd=dd, k=k, x1=x1, x2=x2: e.scalar_tensor_tensor(dd[:, k, 0:16], x1[:], 0.0, x2[:], ALU.max, ALU.add), r=[x1r, x2r], w=[ddr])
                    kb.op("dve", lambda e, dd=dd, k=k: e.tensor_tensor(dd[:, k, 16:32], dd[:, k, 0:16], arate, ALU.mult), r=[ddr, R_pf], w=[ddr])
                kb.dma(zsd[:, t0:t0 + nt_, :], z_[:, 0:nt_, :], r=[zr_], w=[R_ssd])
                kb.dma(dtd[:, t0:t0 + nt_, :], dd[:, 0:nt_, :], r=[ddr], w=[R_ssd])

        mark("SG")
        with stage_scope() as st:
            hs = st.enter_context(nc.sbuf_tensor(U("hs"), [128, 256], F32))
            hsb = st.enter_context(nc.sbuf_tensor(U("hsb"), [128, 256], BF16))
            R_hs, R_hsb = Reg("hs"), Reg("hsb")
            xsl = Rot(st, "xsl", [128, 512], BF16, 2)
            bml = Rot(st, "bml", [128, 128], BF16, 2)
            bcl = Rot(st, "bcl", [128, 2, 128], BF16, 2)
            dtl = Rot(st, "dtl", [128, 32], F32, 2)
            zl = Rot(st, "zl", [128, 512], BF16, 2)
            yl = Rot(st, "yl", [128, 512], F32, 2)
            rA = Rot(st, "rA", [128, 1024], F32, 2)
            LTt = Rot(st, "LTt", [128, 8, 128], F32, 2)
            WTt = Rot(st, "WTt", [128, 8, 128], BF16, 2)
            dA = Rot(st, "dA", [128, 16], F32, 2)
            xdt = Rot(st, "xdt", [128, 512], BF16, 2)
            x32t = Rot(st, "x32t", [128, 512], F32, 2)
            xd32t = Rot(st, "xd32t", [128, 512], F32, 2)
            z32t = Rot(st, "z32t", [128, 512], F32, 2)
            cbs = Rot(st, "cbs", [128, 256], F32, 2)
            xdc = Rot(st, "xdc", [128, 512], BF16, 2)
            yt = Rot(st, "yt", [128, 512], F32, 2)
            yo = Rot(st, "yo", [128, 512], F32, 2)
            sq5 = Rot(st, "sq5", [128, 512], F32, 1)
            s1 = Rot(st, "s1", [128, 1], F32, 2)
            ynb = Rot(st, "ynb", [128, 512], BF16, 2)
            yT = Rot(st, "yT", [128, 4, 128], BF16, 2)
            for d in range(2):
                M1, M2, MK = cst[:, 1 + 3 * d, :], cst[:, 2 + 3 * d, :], cst[:, 3 + 3 * d, :]
                dcol = 127 if d == 0 else 0
                order = list(range(C.CT)) + list(range(C.CT, C.NTILE))
                if d == 1:
                    order = list(range(C.CT - 1, -1, -1)) + list(range(C.NTILE - 1, C.CT - 1, -1))
                kb.op("dve", lambda e: e.memset(hs[:], 0.0), w=[R_hs])
                for ti in order:
                    isctx = ti < C.CT
                    need_y = not (isctx and last)
                    c0 = ti * 128
                    x_, xr_ = xsl.get()
                    kb.dma(x_[:], xsd[:, ti, :], r=[R_ssd], w=[xr_])
                    bm_, bmr = bml.get()
                    kb.dma(bm_[:], bmd[:, ti, :], r=[R_ssd], w=[bmr])
                    bc_, bcr = bcl.get()
                    kb.dma(bc_[:], bcT[:, :, c0:c0 + 128], r=[R_ssd], w=[bcr])
                    dt_, dtr = dtl.get()
                    kb.dma(dt_[:], dtd[:, ti, :], r=[R_ssd], w=[dtr])
                    dtv = dt_[:, d * 8:d * 8 + 8]
                    av = dt_[:, 16 + d * 8:16 + d * 8 + 8]
                    ra, rar = rA.get()
                    kb.op("dve", lambda e, ra=ra, av=av, M2=M2: e.tensor_tensor(ra[:].rearrange("p (h l) -> p h l", l=128), M2.unsqueeze(1).broadcast_to([128, 8, 128]), av.unsqueeze(2).broadcast_to([128, 8, 128]), ALU.mult), r=[dtr, R_cst], w=[rar])
                    pl0, pl0r = psum()
                    pl1, pl1r = psum()
                    kb.op("pe", lambda e, pl0=pl0, ra=ra, M1=M1: e.matmul(pl0[:, :], M1, ra[:, 0:512], start=True, stop=True), r=[rar, R_cst], w=[pl0r])
                    kb.op("pe", lambda e, pl1=pl1, ra=ra, M1=M1: e.matmul(pl1[:, :], M1, ra[:, 512:1024], start=True, stop=True), r=[rar, R_cst], w=[pl1r])
                    pa, par = psum()
                    kb.op("pe", lambda e, pa=pa, av=av, M2=M2: e.matmul(pa[:, 0:8], M2, av, start=True, stop=True), r=[dtr, R_cst], w=[par])
                    kb.op("pe", lambda e, pa=pa, av=av: e.matmul(pa[:, 8:16], onesf[:], av, start=True, stop=True), r=[dtr, R_cst], w=[par])
                    lt_, ltr = LTt.get()
                    kb.op("act", lambda e, lt_=lt_, pl0=pl0: e.activation(out=lt_[:, 0:4, :].rearrange("p h l -> p (h l)"), in_=pl0[:, :], func=AF.Exp), r=[pl0r], w=[ltr])
                    kb.op("act", lambda e, lt_=lt_, pl1=pl1: e.activation(out=lt_[:, 4:8, :].rearrange("p h l -> p (h l)"), in_=pl1[:, :], func=AF.Exp), r=[pl1r], w=[ltr])
                    da_, dar = dA.get()
                    kb.op("act", lambda e, da_=da_, pa=pa: e.activation(out=da_[:], in_=pa[:, 0:16], func=AF.Exp), r=[par], w=[dar])
                    if C.sgl < 2:
                        continue
                    x32, x32r = x32t.get()
                    kb.op("act", lambda e, x32=x32, x_=x_: e.activation(out=x32[:], in_=x_[:], func=AF.Copy), r=[xr_], w=[x32r])
                    xd32, xd32r = xd32t.get()
                    kb.op("dve", lambda e, xd32=xd32, x32=x32, dtv=dtv: e.tensor_tensor(xd32[:].rearrange("p (h q) -> p h q", q=64), x32[:].rearrange("p (h q) -> p h q", q=64), dtv.unsqueeze(2).broadcast_to([128, 8, 64]), ALU.mult), r=[x32r, dtr], w=[xd32r])
                    xd_, xdr = xdt.get()
                    kb.op("act", lambda e, xd_=xd_, xd32=xd32: e.activation(out=xd_[:], in_=xd32[:], func=AF.Copy), r=[xd32r], w=[xdr])
                    if need_y:
                        pcs = [psum(), psum()]
                        for g in range(2):
                            rows = slice(g * 64, g * 64 + 64)
                            kb.op("pe", lambda e, pcs=pcs, g=g, rows=rows, bc_=bc_: e.matmul(pcs[g][0][:, 0:128], bc_[rows, 0, :], bc_[rows, 1, :], start=True, stop=True), r=[bcr], w=[pcs[g][1]])
                        kb.op("dve", lambda e, lt_=lt_, MK=MK: e.tensor_tensor(lt_[:], lt_[:], MK.unsqueeze(1).broadcast_to([128, 8, 128]), ALU.mult), r=[ltr, R_cst], w=[ltr])
                        wt_, wtr = WTt.get()
                        cb_, cbr = cbs.get()
                        for g in range(2):
                            kb.op("act", lambda e, cb_=cb_, pcs=pcs, g=g: e.activation(out=cb_[:, g * 128:(g + 1) * 128], in_=pcs[g][0][:, 0:128], func=AF.Copy), r=[pcs[g][1]], w=[cbr])
                        for g in range(2):
                            kb.op("dve", lambda e, wt_=wt_, lt_=lt_, cb_=cb_, g=g: e.tensor_tensor(wt_[:, 4 * g:4 * g + 4, :], lt_[:, 4 * g:4 * g + 4, :], cb_[:, g * 128:(g + 1) * 128].unsqueeze(1).broadcast_to([128, 4, 128]), ALU.mult), r=[ltr, cbr], w=[wtr])
                        if C.sgl < 3:
                            continue
                        py, pyr = psum()
                        for h in range(8):
                            kb.op("pe", lambda e, py=py, h=h, wt_=wt_, xd_=xd_: e.matmul(py[:, h * 64:(h + 1) * 64], wt_[:, h, :], xd_[:, h * 64:(h + 1) * 64], start=True, stop=True), r=[wtr, xdr], w=[pyr])
                        kb.op("act", lambda e: e.activation(out=hsb[:], in_=hs[:], func=AF.Copy), r=[R_hs], w=[R_hsb])
                        pofs = [psum(), psum()]
                        for g in range(2):
                            rows = slice(g * 64, g * 64 + 64)
                            kb.op("pe", lambda e, pofs=pofs, g=g, rows=rows, bc_=bc_: e.matmul(pofs[g][0][:, 0:256], bc_[rows, 1, :], hsb[rows, :], start=True, stop=True), r=[bcr, R_hsb], w=[pofs[g][1]])
                        yt_, ytr = yt.get()
                        for g in range(2):
                            kb.op("dve", lambda e, yt_=yt_, pofs=pofs, da_=da_, g=g: e.tensor_tensor(yt_[:, g * 256:(g + 1) * 256].rearrange("p (h q) -> p h q", q=64), pofs[g][0][:, 0:256].rearrange("p (h q) -> p h q", q=64), da_[:, 4 * g:4 * g + 4].unsqueeze(2).broadcast_to([128, 4, 64]), ALU.mult), r=[pofs[g][1], dar], w=[ytr])
                        yo_, yor = yo.get()
                        kb.op("dve", lambda e, yo_=yo_, yt_=yt_, py=py: e.tensor_tensor(yo_[:], yt_[:], py[:, :], ALU.add), r=[ytr, pyr], w=[yor])
                        if d == 0:
                            kb.op("dve", lambda e, yt_=yt_, x32=x32: e.tensor_tensor(yt_[:].rearrange("p (h q) -> p h q", q=64), x32[:].rearrange("p (h q) -> p h q", q=64), dsk.unsqueeze(2).broadcast_to([128, 8, 64]), ALU.mult), r=[x32r, R_pf, yor], w=[ytr])
                            kb.op("pool", lambda e, yo_=yo_, yt_=yt_: e.tensor_tensor(yo_[:], yo_[:], yt_[:], ALU.add), r=[ytr, yor], w=[yor])
                            kb.dma(yd[:, ti, :], yo_[:], r=[yor], w=[R_yd])
                        else:
                            y_, yr_ = yl.get()
                            kb.dma(y_[:], yd[:, ti, :], r=[R_yd], w=[yr_])
                            z_, zr_ = zl.get()
                            kb.dma(z_[:], zsd[:, ti, :], r=[R_ssd], w=[zr_])
                            kb.op("pool", lambda e, yo_=yo_, y_=y_: e.tensor_tensor(yo_[:], yo_[:], y_[:], ALU.add), r=[yr_, yor], w=[yor])
                            z32, z32r = z32t.get()
                            kb.op("act", lambda e, z32=z32, z_=z_: e.activation(out=z32[:], in_=z_[:], func=AF.Copy), r=[zr_], w=[z32r])
                            kb.op("dve", lambda e, yo_=yo_, z32=z32: e.tensor_tensor(yo_[:], yo_[:], z32[:], ALU.mult), r=[z32r, yor], w=[yor])
                            q5, q5r = sq5.get()
                            kb.op("dve", lambda e, q5=q5, yo_=yo_: e.tensor_tensor(q5[:], yo_[:], yo_[:], ALU.mult), r=[yor], w=[q5r])
                            s_, sr_ = s1.get()
                            kb.op("dve", lambda e, s_=s_, q5=q5: e.tensor_reduce(s_[:], q5[:], AX.X, ALU.add), r=[q5r], w=[sr_])
                            rsqrt(s_[:], s_[:], 512 * C.EPS, [sr_], [sr_])
                            yb_, ybr = ynb.get()
                            kb.op("dve", lambda e, yb_=yb_, yo_=yo_, s_=s_: e.scalar_tensor_tensor(yb_[:], yo_[:], s_[:, 0:1], g512, ALU.mult, ALU.mult), r=[yor, sr_, R_pf], w=[ybr])
                            ptp, ptr = psum()
                            pb = ptp[:, :].bitcast(BF16)
                            for k in range(4):
                                kb.op("pe", lambda e, pb=pb, k=k, yb_=yb_: e.transpose(pb[:, k * 128:(k + 1) * 128], yb_[:, k * 128:(k + 1) * 128], identb[:]), r=[ybr, R_cst], w=[ptr])
                            yT_, yTr = yT.get()
                            kb.op("act", lambda e, yT_=yT_, pb=pb: e.activation(out=yT_[:], in_=pb[:, 0:512].rearrange("p (k t) -> p k t", t=128), func=AF.Copy), r=[ptr], w=[yTr])
                            kb.dma(bsd[:, :, c0:c0 + 128], yT_[:], r=[yTr], w=[R_bsd])
                    if C.sgl < 4:
                        continue
                    xc_, xcr = xdc.get()
                    kb.op("dve", lambda e, xc_=xc_, xd32=xd32, lt_=lt_: e.tensor_tensor(xc_[:].rearrange("p (h q) -> p h q", q=64), xd32[:].rearrange("p (h q) -> p h q", q=64), lt_[:, :, dcol:dcol + 1].broadcast_to([128, 8, 64]), ALU.mult), r=[xd32r, ltr], w=[xcr])
                    pS_, pSr = psum()
                    for g in range(2):
                        kb.op("pe", lambda e, pS_=pS_, g=g, bm_=bm_, xc_=xc_: e.matmul(pS_[:, g * 256:(g + 1) * 256], bm_[:, :], xc_[:, g * 256:(g + 1) * 256], start=True, stop=True), r=[bmr, xcr], w=[pSr])
                    for g in range(2):
                        rows = slice(g * 64, g * 64 + 64)
                        kb.op("dve", lambda e, g=g, rows=rows, da_=da_: e.tensor_tensor(hs[rows, :].rearrange("p (h q) -> p h q", q=64), hs[rows, :].rearrange("p (h q) -> p h q", q=64), da_[rows, 8 + 4 * g:12 + 4 * g].unsqueeze(2).broadcast_to([64, 4, 64]), ALU.mult), r=[dar, R_hs, R_hsb], w=[R_hs])
                        kb.op("dve", lambda e, g=g, rows=rows, pS_=pS_: e.tensor_tensor(hs[rows, :], hs[rows, :], pS_[rows, g * 256:(g + 1) * 256], ALU.add), r=[pSr, R_hs], w=[R_hs])

        mark("SH1")
        with stage_scope() as st:
            stg = Rot(st, "wstg", [128, 128], F32, 4)
            hbk = Rot(st, "hbk", [128, DC, 512], BF16)
            atb = Rot(st, "atb", [64, 8, 512], BF16)
            brb = Rot(st, "brb", [128, 3, 4, 512], BF16)
            sg = Rot(st, "sg", [128, 512], F32, 2)
            acc = Rot(st, "acc", [128, 512], F32, 2)
            tq = Rot(st, "tq", [128, 512], F32, 2)
            mo = Rot(st, "mo", [128, 512], BF16, 2)
            for dc in range(DC):
                with ExitStack() as st2:
                    wg, wgr = [], []
                    for k in range(4):
                        a, b = load_w(st2, "wg%d" % k, wview(w_in, li, OFF["gl"] + k * D + dc * 128, 128), [128, DC, 128], stg)
                        wg.append(a); wgr.append(b)
                    wa, war = load_w(st2, "wba", w_branch[li, 0, :, dc * 128:(dc + 1) * 128].rearrange("(h p) n -> p h n", p=64), [64, 8, 128], stg)
                    wb_, wbr = [], []
                    for k in range(1, 4):
                        a, b = load_w(st2, "wb%d" % k, w_branch[li, k, :, dc * 128:(dc + 1) * 128].rearrange("(c p) n -> p c n", p=128), [128, 4, 128], stg)
                        wb_.append(a); wbr.append(b)
                    for seg, c0, n, f, l_ in blocks(512):
                        if last and seg == 1:
                            continue
                        ht, hr = hbk.get()
                        kb.dma(ht[:, :, 0:n], hd[:, :, C.ext(c0):C.ext(c0) + n], r=[R_hd], w=[hr])
                        at_, atr = atb.get()
                        kb.dma(at_[:, :, 0:n], attd[:, :, c0:c0 + n], r=[R_attd], w=[atr])
                        br_, brr = brb.get()
                        for k, (src, rr) in enumerate(((bcd, R_bcd), (bpd, R_bpd), (bsd, R_bsd))):
                            kb.dma(br_[:, k, :, 0:n], src[:, :, c0:c0 + n], r=[rr], w=[brr])
                        ac_, acr = acc.get()
                        for k in range(4):
                            pg, pgr = psum()
                            pp, ppr = psum()
                            for c in range(DC):
                                kb.op("pe", lambda e, k=k, c=c, pg=pg, ht=ht: e.matmul(pg[:, 0:n], wg[k][:, c, :], ht[:, c, 0:n], start=(c == 0), stop=(c == DC - 1)), r=[hr, wgr[k]], w=[pgr])
                            if k == 0:
                                for h in range(8):
                                    kb.op("pe", lambda e, h=h, pp=pp, at_=at_: e.matmul(pp[:, 0:n], wa[:, h, :], at_[:, h, 0:n], start=(h == 0), stop=(h == 7)), r=[atr, war], w=[ppr])
                            else:
                                for c in range(4):
                                    kb.op("pe", lambda e, k=k, c=c, pp=pp, br_=br_: e.matmul(pp[:, 0:n], wb_[k - 1][:, c, :], br_[:, k - 1, c, 0:n], start=(c == 0), stop=(c == 3)), r=[brr, wbr[k - 1]], w=[ppr])
                            s_, sr_ = sg.get()
                            kb.op("act", lambda e, s_=s_, pg=pg: e.activation(out=s_[:, 0:n], in_=pg[:, 0:n], func=AF.Sigmoid), r=[pgr], w=[sr_])
                            if k == 0:
                                kb.op("dve", lambda e, ac_=ac_, s_=s_, pp=pp: e.tensor_tensor(ac_[:, 0:n], s_[:, 0:n], pp[:, 0:n], ALU.mult), r=[sr_, ppr], w=[acr])
                            else:
                                t_, tr_ = tq.get()
                                kb.op("dve", lambda e, t_=t_, s_=s_, pp=pp: e.tensor_tensor(t_[:, 0:n], s_[:, 0:n], pp[:, 0:n], ALU.mult), r=[sr_, ppr], w=[tr_])
                                kb.op("pool", lambda e, ac_=ac_, t_=t_: e.tensor_tensor(ac_[:, 0:n], ac_[:, 0:n], t_[:, 0:n], ALU.add), r=[tr_, acr], w=[acr])
                        m_, mr_ = mo.get()
                        kb.op("act", lambda e, m_=m_, ac_=ac_: e.activation(out=m_[:, 0:n], in_=ac_[:, 0:n], func=AF.Copy), r=[acr], w=[mr_])
                        kb.dma(md[:, dc, c0:c0 + n], m_[:, 0:n], r=[mr_], w=[R_md])
                    kb.full_barrier()

        mark("SH2")
        with stage_scope() as st:
            stg = Rot(st, "wstg", [128, D], F32)
            wo, wor = load_w(st, "wo", wview(w_out, li, 0, D), [128, DC, D], stg)
            rots = norm_rots(st, 512)
            xb = Rot(st, "xb", [128, DC, 512], F32)
            mb = Rot(st, "mb", [128, DC, 512], BF16)
            for seg, c0, n, f, l_ in blocks(512):
                if last and seg == 1:
                    continue
                xt, xr = xb.get()
                kb.dma(xt[:, :, 0:n], xd[:, :, c0:c0 + n], r=[R_xd], w=[xr])
                mt, mr = mb.get()
                kb.dma(mt[:, :, 0:n], md[:, :, c0:c0 + n], r=[R_md], w=[mr])
                for dc in range(DC):
                    py, pyr = psum()
                    for c in range(DC):
                        kb.op("pe", lambda e, c=c, dc=dc, py=py, mt=mt: e.matmul(py[:, 0:n], wo[:, c, dc * 128:(dc + 1) * 128], mt[:, c, 0:n], start=(c == 0), stop=(c == DC - 1)), r=[mr, wor], w=[pyr])
                    kb.op("dve", lambda e, dc=dc, py=py, xt=xt, seg=seg: e.scalar_tensor_tensor(xt[:, dc, 0:n], py[:, 0:n], AB[:, 2, dc, seg:seg + 1], xt[:, dc, 0:n], ALU.mult, ALU.add), r=[pyr, R_mod, xr], w=[xr])
                kb.dma(xd[:, :, c0:c0 + n], xt[:, :, 0:n], r=[xr], w=[R_xd])
                norm_block(rots, xt, xr, c0, n, seg, 3, 4)

        mark("SI")
        with stage_scope() as st:
            SBW = min(C.SBW, T)
            GJ = 4
            stg = Rot(st, "wstg", [128, 1024], F32, 3)
            hx = st.enter_context(nc.sbuf_tensor(U("hx"), [128, DC, SBW + 2], BF16))
            R_hx = Reg("hx")
            y2 = st.enter_context(nc.sbuf_tensor(U("y2"), [128, DC, SBW], F32))
            R_y2 = Reg("y2")
            actT = st.enter_context(nc.sbuf_tensor(U("actT"), [128, GJ, SBW], BF16))
            R_act = Reg("actT")
            upr = Rot(st, "upr", [128, 2, SBW + 2], BF16, 1)
            cg_ = Rot(st, "cgt", [128, SBW], F32, 1)
            cv_ = Rot(st, "cvt", [128, SBW], F32, 1)
            wup = Rot(st, "wup", [128, DC, 256], BF16, 2)
            wdn = Rot(st, "wdn", [128, GJ, D], BF16, 1)
            xb = Rot(st, "xb1", [128, 512], F32, 3)
            sblocks = []
            R_xd2, R_xd3 = Reg("xd2"), Reg("xd3")
            for seg, s0, ln in C.segs():
                if last and seg == 1:
                    continue
                for c in range(0, ln, SBW):
                    sblocks.append((seg, s0 + c, min(SBW, ln - c)))
            for seg, s0, m in sblocks:
                kb.dma(hx[:, :, 0:m + 2], hd[:, :, C.ext(s0) - 1:C.ext(s0) + m + 1], r=[R_hd], w=[R_hx])
                subs = [(c, min(512, m + 2 - c)) for c in range(0, m + 2, 512)]
                subs2 = [(c, min(512, m - c)) for c in range(0, m, 512)]
                for j0 in range(0, FC, GJ):
                    gj = min(GJ, FC - j0)
                    for jj in range(gj):
                        j = j0 + jj
                        wu, wur = wup.get()
                        for k, col in enumerate((j * 128, DFF + j * 128)):
                            s_t, s_r = stg.get()
                            kb.dma(s_t[:, 0:DC * 128].rearrange("p (c n) -> p c n", n=128), wview(w_up, li, col, 128), w=[s_r])
                            kb.op("pool" if k else "dve", lambda e, wu=wu, k=k, s_t=s_t: e.tensor_copy(wu[:, :, k * 128:(k + 1) * 128], s_t[:, 0:DC * 128].rearrange("p (c n) -> p c n", n=128)), r=[s_r], w=[wur])
                        up_, upr_ = upr.get()
                        for k in range(2):
                            for (c, nn) in subs:
                                pu_, pur = psum()
                                for cc in range(DC):
                                    kb.op("pe", lambda e, cc=cc, k=k, c=c, nn=nn, pu_=pu_, wu=wu: e.matmul(pu_[:, 0:nn], wu[:, cc, k * 128:(k + 1) * 128], hx[:, cc, c:c + nn], start=(cc == 0), stop=(cc == DC - 1)), r=[R_hx, wur], w=[pur])
                                kb.op("act", lambda e, k=k, c=c, nn=nn, pu_=pu_, up_=up_: e.activation(out=up_[:, k, c:c + nn], in_=pu_[:, 0:nn], func=AF.Copy), r=[pur], w=[upr_])
                        outs = []
                        for k, (rot, eng) in enumerate(((cg_, "dve"), (cv_, "dve"))):
                            t_, tr_ = rot.get()
                            w3 = pf[:, P_FC + 3 * (k * FC + j):P_FC + 3 * (k * FC + j) + 3]
                            kb.op(eng, lambda e, t_=t_, up_=up_, k=k, w3=w3: e.tensor_scalar(t_[:, 0:m], up_[:, k, 1:m + 1], w3[:, 1:2], None, ALU.mult), r=[upr_, R_pf], w=[tr_])
                            kb.op(eng, lambda e, t_=t_, up_=up_, k=k, w3=w3: e.scalar_tensor_tensor(t_[:, 0:m], up_[:, k, 0:m], w3[:, 0:1], t_[:, 0:m], ALU.mult, ALU.add), r=[upr_, R_pf, tr_], w=[tr_])
                            kb.op(eng, lambda e, t_=t_, up_=up_, k=k, w3=w3: e.scalar_tensor_tensor(t_[:, 0:m], up_[:, k, 2:m + 2], w3[:, 2:3], t_[:, 0:m], ALU.mult, ALU.add), r=[upr_, R_pf, tr_], w=[tr_])
                            outs.append((t_, tr_))
                        (tg, tgr), (tv, tvr) = outs
                        kb.op("act", lambda e, tg=tg: e.activation(out=tg[:, 0:m], in_=tg[:, 0:m], func=AF.Silu), r=[tgr], w=[tgr])
                        kb.op("dve", lambda e, jj=jj, tg=tg, tv=tv: e.tensor_tensor(actT[:, jj, 0:m], tg[:, 0:m], tv[:, 0:m], ALU.mult), r=[tgr, tvr], w=[R_act])
                    wd, wdr = wdn.get()
                    for jj in range(gj):
                        s_t, s_r = stg.get()
                        kb.dma(s_t[:, 0:D], w_down[li, (j0 + jj) * 128:(j0 + jj + 1) * 128, :], w=[s_r])
                        kb.op("pool", lambda e, wd=wd, jj=jj, s_t=s_t: e.tensor_copy(wd[:, jj, :], s_t[:, 0:D]), r=[s_r], w=[wdr])
                    for dc in range(DC):
                        for (c, nn) in subs2:
                            py, pyr = psum()
                            for jj in range(gj):
                                kb.op("pe", lambda e, jj=jj, dc=dc, c=c, nn=nn, py=py, wd=wd: e.matmul(py[:, 0:nn], wd[:, jj, dc * 128:(dc + 1) * 128], actT[:, jj, c:c + nn], start=(jj == 0), stop=(jj == gj - 1)), r=[R_act, wdr], w=[pyr])
                            if j0 == 0:
                                kb.op("act", lambda e, dc=dc, c=c, nn=nn, py=py: e.activation(out=y2[:, dc, c:c + nn], in_=py[:, 0:nn], func=AF.Copy), r=[pyr], w=[R_y2])
                            else:
                                kb.op("dve", lambda e, dc=dc, c=c, nn=nn, py=py: e.tensor_tensor(y2[:, dc, c:c + nn], y2[:, dc, c:c + nn], py[:, 0:nn], ALU.add), r=[pyr, R_y2], w=[R_y2])
                for (c, nn) in subs2:
                    for dc in range(DC):
                        xt, xr = xb.get()
                        kb.dma(xt[:, 0:nn], xd[:, dc, s0 + c:s0 + c + nn], r=[R_xd2], w=[xr])
                        kb.op("dve", lambda e, dc=dc, c=c, nn=nn, xt=xt, seg=seg: e.scalar_tensor_tensor(xt[:, 0:nn], y2[:, dc, c:c + nn], AB[:, 5, dc, seg:seg + 1], xt[:, 0:nn], ALU.mult, ALU.add), r=[R_y2, R_mod, xr], w=[xr])
                        kb.dma(xd[:, dc, s0 + c:s0 + c + nn], xt[:, 0:nn], r=[xr], w=[R_xd3])

    mark("EPI")
    with stage_scope() as st:
        fn = st.enter_context(nc.sbuf_tensor(U("fn"), [128, D], F32))
        R_fn = Reg("fn")
        kb.dma(fn[:], fnorm_in[:, :], w=[R_fn])
        kb.op("dve", lambda e: e.tensor_scalar(fn[:], fn[:], float(math.sqrt(D)), None, ALU.mult), r=[R_fn], w=[R_fn])
        xb = Rot(st, "xb", [128, DC, 128], F32)
        xtm = Rot(st, "xtm", [128, D], F32)
        sqq = Rot(st, "sqq", [128, D], F32, 1)
        s1 = Rot(st, "s1", [128, 1], F32, 2)
        ob = Rot(st, "ob", [128, D], F32)
        for ti in range(C.LT):
            c0 = CTX + ti * 128
            xt, xr = xb.get()
            kb.dma(xt[:], xd[:, :, c0:c0 + 128], r=[R_xd], w=[xr])
            xm, xmr = xtm.get()
            for c4 in range(0, DC, 4):
                pst, psr = psum()
                nn = min(4, DC - c4)
                for c in range(c4, c4 + nn):
                    kb.op("pe", lambda e, pst=pst, xt=xt, c=c, c4=c4: e.transpose(pst[:, (c - c4) * 128:(c - c4 + 1) * 128], xt[:, c, :], identf), r=[xr, R_cst], w=[psr])
                kb.op("act", lambda e, pst=pst, xm=xm, c4=c4, nn=nn: e.activation(out=xm[:, c4 * 128:(c4 + nn) * 128], in_=pst[:, 0:nn * 128], func=AF.Copy), r=[psr], w=[xmr])
            q_, qr_ = sqq.get()
            kb.op("dve", lambda e, q_=q_, xm=xm: e.tensor_tensor(q_[:], xm[:], xm[:], ALU.mult), r=[xmr], w=[qr_])
            s_, sr_ = s1.get()
            kb.op("dve", lambda e, s_=s_, q_=q_: e.tensor_reduce(s_[:], q_[:], AX.X, ALU.add), r=[qr_], w=[sr_])
            rsqrt(s_[:], s_[:], D * C.EPS, [sr_], [sr_])
            o_, or_ = ob.get()
            kb.op("dve", lambda e, o_=o_, xm=xm, s_=s_: e.scalar_tensor_tensor(o_[:], xm[:], s_[:, 0:1], fn[:], ALU.mult, ALU.mult), r=[xmr, sr_, R_fn], w=[or_])
            kb.dma(out[ti * 128:(ti + 1) * 128, :], o_[:], r=[or_], w=[R_out])
    kb.full_barrier()
    cnt = kb.emit()
    es.close()
    return nc, cnt


def _fm(v, nchunk):
    return np.ascontiguousarray(np.asarray(v, np.float32).reshape(nchunk, 128).T)


def _rep(v):
    v = np.asarray(v, np.float32).reshape(1, -1)
    return np.ascontiguousarray(np.broadcast_to(v, (128, v.shape[1])))


def pack_inputs(cfg, inp, b):
    C = cfg
    L, DC, FC = C.DEPTH, C.DC, C.FC
    f32 = np.float32
    m = {}
    m["x"] = np.ascontiguousarray(inp["x"][b], dtype=f32)
    m["ctx"] = np.ascontiguousarray(inp["ctx"][b], dtype=f32)
    m["cond"] = np.ascontiguousarray(np.stack([_fm(inp["c"][b], DC), _fm(inp["c_ctx"], DC)], axis=-1))
    for k in ("w_mod", "w_in", "pool_w", "w_branch", "w_out", "w_up", "w_down"):
        m[k] = np.ascontiguousarray(inp[k], dtype=f32)
    pf, pt = [], []
    for l in range(L):
        cols = [_fm(inp["b_mod"][l], 6 * DC), _fm(inp["norm_mix"][l], DC), _fm(inp["norm_ffn"][l], DC)]
        cs = np.asarray(inp["conv_short"][l], f32)
        cols.append(np.ascontiguousarray(cs.reshape(3, 4, 128).transpose(2, 1, 0)).reshape(128, 12))
        cols.append(_fm(inp["pool_scale"][l], 4))
        sw = np.asarray(inp["ssd_conv_w"][l], f32)
        cols.append(np.ascontiguousarray(sw.reshape(3, 6, 128).transpose(2, 1, 0)).reshape(128, 18))
        cols.append(_fm(inp["ssd_conv_b"][l], 6))
        fc = np.asarray(inp["ffn_conv"][l], f32)
        cols.append(np.ascontiguousarray(fc.reshape(3, 2 * FC, 128).transpose(2, 1, 0)).reshape(128, 6 * FC))
        pf.append(np.concatenate(cols, axis=1))
        tcols = [_rep(np.tile(np.asarray(inp["q_norm"][l], f32), 8)), _rep(np.tile(np.asarray(inp["k_norm"][l], f32), 2)),
                 _rep(inp["ssd_norm"][l]), _rep(np.asarray(inp["ssd_dt_bias"][l], f32).reshape(-1)),
                 _rep(np.asarray(inp["ssd_a_log"][l], f32).reshape(-1)), _rep(inp["ssd_d"][l])]
        pt.append(np.concatenate(tcols, axis=1))
    m["pf"] = np.ascontiguousarray(np.stack(pf), dtype=f32)
    m["pt"] = np.ascontiguousarray(np.stack(pt), dtype=f32)
    m["fnorm"] = _rep(inp["final_norm"])
    t = np.arange(C.SEQ)
    row, col = (t // C.GRID_W).astype(f32), (t % C.GRID_W).astype(f32)
    inv = (np.float32(10000.0) ** (-np.arange(16, dtype=f32) / np.float32(16))).astype(f32)
    ang = np.stack([row[:, None] * inv, col[:, None] * inv], axis=1).astype(f32)
    tab = np.concatenate([np.cos(ang).reshape(C.SEQ, 32), np.sin(ang).reshape(C.SEQ, 32)], axis=1).astype(f32)
    m["rope"] = np.ascontiguousarray(tab.reshape(C.LT, 128, 64).transpose(1, 0, 2))
    invc = np.zeros((128, 4, 2, 16), f32)
    for g, w in enumerate((2, 4, 8, 16)):
        for seg, n in ((0, C.SEQ), (1, C.CTX)):
            for i in range(8):
                for (tt, slot) in ((i, i), (n - 8 + i, 8 + i)):
                    lo, hi = max(tt - w // 2, 0), min(tt + w // 2, n)
                    invc[:, g, seg, slot] = 1.0 / float(hi - lo)
    m["invc"] = invc
    k = np.arange(128)
    cst = np.zeros((128, 7, 128), f32)
    cst[:, 0, :] = np.eye(128)
    cst[:, 1, :] = (k[:, None] > k[None, :])
    cst[:, 2, :] = (k[:, None] <= k[None, :])
    cst[:, 3, :] = (k[None, :] >= k[:, None])
    cst[:, 4, :] = (k[:, None] < k[None, :])
    cst[:, 5, :] = (k[:, None] >= k[None, :])
    cst[:, 6, :] = (k[:, None] >= k[None, :])
    m["cst"] = cst
    return m


_CACHE = {}


def kernel(**inputs):
    cfg = Cfg()
    if "nc" not in _CACHE:
        _CACHE["nc"] = build_program(cfg)[0]
    nc = _CACHE["nc"]
    nb = inputs["x"].shape[0]
    in_maps = [pack_inputs(cfg, inputs, b) for b in range(nb)]
    res = run_bass_kernel_spmd(nc, in_maps, core_ids=list(range(nb)))
    return np.stack([np.asarray(res.results[b]["out"], dtype=np.float32) for b in range(nb)], axis=0)
```

```python
from concourse.bass_utils import run_bass_kernel_spmd
import numpy as np
import concourse.bass as bass
import concourse.mybir as mybir
from contextlib import ExitStack

F32 = mybir.dt.float32
BF16 = mybir.dt.bfloat16
AF = mybir.ActivationFunctionType
ALU = mybir.AluOpType
AX = mybir.AxisListType


import types


def _freeze(fn):
    if fn is None or fn.__closure__ is None:
        return fn
    cells = tuple(types.CellType(c.cell_contents) for c in fn.__closure__)
    return types.FunctionType(fn.__code__, fn.__globals__, fn.__name__, fn.__defaults__, cells)


class Reg:
    __slots__ = ("name", "w", "rd")

    def __init__(self, name=""):
        self.name = name
        self.w = None
        self.rd = {}


class Op:
    __slots__ = ("en", "fn", "waits", "needed", "seq", "val", "dma", "dsem", "dval")

    def __init__(self, en, fn):
        self.en, self.fn = en, fn
        self.waits = []
        self.needed = False
        self.seq = 0
        self.val = None
        self.dma = False
        self.dsem = None
        self.dval = 0


class KB:
    ENGS = ("pe", "act", "dve", "pool", "sp")

    def __init__(self, nc, es, n_dma_sems=24):
        self.nc, self.es = nc, es
        self.eng = {"pe": nc.tensor, "act": nc.scalar, "dve": nc.vector, "pool": nc.gpsimd, "sp": nc.sync}
        self.sem = {e: es.enter_context(nc.semaphore("s_" + e)) for e in self.ENGS}
        self.ops = []
        self.nseq = {e: 0 for e in self.ENGS}
        self.seen = {e: {} for e in self.ENGS}
        self.seen_d = {e: {} for e in self.ENGS}
        self.dsems = [es.enter_context(nc.semaphore("d%d" % i)) for i in range(n_dma_sems)]
        self.dlast = [None] * n_dma_sems
        self.dval = [0] * n_dma_sems
        self.drr = 0
        self.drr2 = 0
        self.last = {}
        self.mute = False

    @staticmethod
    def is_dram(ap):
        return "DRAM" in str(ap.space).upper() or "DRAM" in type(ap.tensor).__name__.upper()

    def _need(self, op, dep):
        if dep is None:
            return
        E = op.en
        if dep.dma:
            i = dep.dsem
            if self.seen_d[E].get(i, 0) >= dep.dval:
                return
            self.seen_d[E][i] = dep.dval
            op.waits.append(("d", i, dep.dval))
        else:
            if dep.en == "pe" and E == "pe":
                return
            if self.seen[E].get(dep.en, 0) >= dep.seq:
                return
            self.seen[E][dep.en] = dep.seq
            dep.needed = True
            op.waits.append(("e", dep))

    def _deps(self, op, r, w):
        for reg in r:
            self._need(op, reg.w)
        for reg in w:
            self._need(op, reg.w)
            for rd in reg.rd.values():
                self._need(op, rd)
        for reg in r:
            reg.rd[op.en if not op.dma else ("d", id(op))] = op
        for reg in w:
            reg.w = op
            reg.rd = {}

    def op(self, en, fn, r=(), w=()):
        if self.mute:
            return None
        o = Op(en, _freeze(fn))
        self.nseq[en] += 1
        o.seq = self.nseq[en]
        self._deps(o, r, w)
        self.ops.append(o)
        self.last[en] = o
        return o

    def dma(self, out, in_, r=(), w=(), q=None, **kw):
        if self.mute:
            return None
        if q is None:
            q = "pool" if self.is_dram(out) else "sp"
        o = Op(q, lambda e: e.dma_start(out=out, in_=in_, **kw))
        o.dma = True
        self.nseq[q] += 1
        o.seq = self.nseq[q]
        nsp = (2 * len(self.dsems)) // 3
        if q == "sp":
            i = self.drr % nsp
            self.drr += 1
        else:
            i = nsp + self.drr2 % (len(self.dsems) - nsp)
            self.drr2 += 1
        prev = self.dlast[i]
        if prev is not None:
            self._need(o, prev)
        self._deps(o, r, w)
        self.dval[i] += 16
        o.dsem, o.dval = i, self.dval[i]
        self.dlast[i] = o
        self.ops.append(o)
        return o

    def full_barrier(self):
        lasts = [self.last.get(e) for e in self.ENGS]
        dl = [d for d in self.dlast if d is not None]
        for en in self.ENGS:
            o = Op(en, None)
            self.nseq[en] += 1
            o.seq = self.nseq[en]
            for d in lasts:
                if d is not None:
                    self._need(o, d)
            for d in dl:
                self._need(o, d)
            self.ops.append(o)

    def barrier(self, regs):
        o = Op("sp", None)
        self.nseq["sp"] += 1
        o.seq = self.nseq["sp"]
        for reg in regs:
            self._need(o, reg.w)
        self.ops.append(o)

    def emit(self):
        cnt = {e: 0 for e in self.ENGS}
        for o in self.ops:
            e = self.eng[o.en]
            for wt in o.waits:
                if wt[0] == "d":
                    e.wait_ge(self.dsems[wt[1]], wt[2])
                else:
                    d = wt[1]
                    assert d.val is not None, "dep emitted after consumer"
                    e.wait_ge(self.sem[d.en], d.val)
            if o.fn is None:
                continue
            ins = o.fn(e)
            if o.dma:
                ins.then_inc(self.dsems[o.dsem], 16)
            elif o.needed:
                cnt[o.en] += 1
                o.val = cnt[o.en]
                ins.then_inc(self.sem[o.en], 1)
        return cnt


import math


class Cfg:
    def __init__(self, D=1024, SEQ=8192, CTX=256, DFF=2816, GRID_W=64, DEPTH=2, SBW=2048):
        self.D, self.SEQ, self.CTX, self.DFF, self.GRID_W, self.DEPTH = D, SEQ, CTX, DFF, GRID_W, DEPTH
        self.DC = D // 128
        self.FC = DFF // 128
        self.T = SEQ
        self.NT = CTX + SEQ
        self.PAD = 8
        self.NE = self.NT + 4 * self.PAD
        self.CT = CTX // 128
        self.LT = SEQ // 128
        self.NTILE = self.CT + self.LT
        self.IN = 4112 + 4 * D
        self.EPS = 1e-6
        self.SBW = SBW
        self.stages = None
        self.sbl = 9
        self.sgl = 9

    def ext(self, col):
        return col + self.PAD if col < self.CTX else col + 3 * self.PAD

    def segs(self):
        return [(1, 0, self.CTX), (0, self.CTX, self.T)]


OFF = dict(q=0, k=512, v=640, bg=768, cg=1280, u=1792, pu=2304, z=2816, xbc=3328, dt=4096, gl=4112)


def build_program(cfg):
    nc = bass.Bass("TRN2", target_bir_lowering=False)
    C = cfg
    D, DC, NT, NE, T, CTX, FC, DFF = C.D, C.DC, C.NT, C.NE, C.T, C.CTX, C.FC, C.DFF
    L = C.DEPTH
    es = ExitStack()
    kb = KB(nc, es)

    def din(name, shape, dt=F32):
        return nc.dram_tensor(name, list(shape), dt, kind="ExternalInput").ap()

    def dscr(name, shape, dt):
        return nc.dram_tensor(name, list(shape), dt, kind="Internal").ap()

    x_in = din("x", [T, D])
    ctx_in = din("ctx", [CTX, D])
    cond_in = din("cond", [128, DC, 2])
    w_mod = din("w_mod", [L, D, 6 * D])
    w_in = din("w_in", [L, D, C.IN])
    pool_w = din("pool_w", [L, 4, 128, 128])
    w_branch = din("w_branch", [L, 4, 512, D])
    w_out = din("w_out", [L, D, D])
    w_up = din("w_up", [L, D, 2 * DFF])
    w_down = din("w_down", [L, DFF, D])
    NPF = 6 * DC + 2 * DC + 12 + 4 + 18 + 6 + 6 * FC
    pf_in = din("pf", [L, 128, NPF])
    NPT = 512 + 128 + 512 + 16 + 16 + 8
    pt_in = din("pt", [L, 128, NPT])
    fnorm_in = din("fnorm", [128, D])
    rope_in = din("rope", [128, C.LT, 64])
    invc_in = din("invc", [128, 4, 2, 16])
    cst_in = din("cst", [128, 7, 128])
    out = nc.dram_tensor("out", [T, D], F32, kind="ExternalOutput").ap()

    xd = dscr("xd", [128, DC, NT], F32)
    hd = dscr("hd", [128, DC, NE], BF16)
    qd = dscr("qd", [128, 4, NT], BF16)
    attd = dscr("attd", [64, 8, NT], BF16)
    bcd = dscr("bcd", [128, 4, NT], BF16)
    bpd = dscr("bpd", [128, 4, NT], BF16)
    bsd = dscr("bsd", [128, 4, NT], BF16)
    md = dscr("md", [128, DC, NT], BF16)
    xsd = dscr("xsd", [128, C.NTILE, 512], BF16)
    zsd = dscr("zsd", [128, C.NTILE, 512], BF16)
    bmd = dscr("bmd", [128, C.NTILE, 128], BF16)
    bcT = dscr("bcT", [128, 2, NT], BF16)
    dtd = dscr("dtd", [128, C.NTILE, 32], F32)
    yd = dscr("yd", [128, C.NTILE, 512], F32)
    R_xd, R_hd, R_qd, R_attd, R_bcd, R_bpd, R_bsd, R_md = [Reg(n) for n in "xd hd qd attd bcd bpd bsd md".split()]
    R_ssd = Reg("ssdprep")
    R_yd = Reg("yd")
    R_out = Reg("out")

    _uc = [0]

    def U(name):
        _uc[0] += 1
        return "t%d_%s" % (_uc[0], name)

    def sb(name, shape, dt):
        return es.enter_context(nc.sbuf_tensor(U(name), list(shape), dt))

    psb = [es.enter_context(nc.psum_tensor("ps%d" % i, [128, 512], F32)) for i in range(8)]
    R_ps = [Reg("ps%d" % i) for i in range(8)]
    pctr = [0]

    def psum():
        i = pctr[0] % 8
        pctr[0] += 1
        return psb[i], R_ps[i]

    class PsRot:
        def __init__(self, idx):
            self.idx, self.i = list(idx), 0

        def get(self):
            k = self.idx[self.i % len(self.idx)]
            self.i += 1
            return psb[k], R_ps[k]

    cst = sb("cst", [128, 7, 128], F32)
    R_cst = Reg("cst")
    kb.dma(cst[:], cst_in[:, :, :], w=[R_cst])
    identb = sb("identb", [128, 128], BF16)
    onesb = sb("onesb", [128, 128], BF16)
    zerob = sb("zerob", [128, DC, 16], BF16)
    kb.op("dve", lambda e: e.tensor_copy(identb[:], cst[:, 0, :]), r=[R_cst], w=[R_cst])
    kb.op("dve", lambda e: e.memset(onesb[:], 1.0), w=[R_cst])
    kb.op("dve", lambda e: e.memset(zerob[:], 0.0), w=[R_cst])
    identf = cst[:, 0, :]
    onesf = sb("onesf", [128, 128], F32)
    kb.op("pool", lambda e: e.memset(onesf[:], 1.0), w=[R_cst])
    cond = sb("cond", [128, DC, 2], F32)
    conds = sb("conds", [128, DC, 2], F32)
    R_cond = Reg("cond")
    kb.dma(cond[:], cond_in[:, :, :], w=[R_cond])
    kb.op("act", lambda e: e.activation(out=conds[:], in_=cond[:], func=AF.Silu), r=[R_cond], w=[R_cond])
    for c0 in (0, C.PAD + CTX, 2 * C.PAD + CTX, 3 * C.PAD + NT):
        kb.dma(hd[:, :, c0:c0 + C.PAD], zerob[:, :, 0:C.PAD], r=[R_cst], w=[R_hd])

    pf = sb("pf", [128, NPF], F32)
    pt = sb("pt", [128, NPT], F32)
    R_pf = Reg("pf")
    modT = sb("modT", [128, 6 * DC, 2], F32)
    AB = sb("AB", [128, 6, DC, 2], F32)
    R_mod = Reg("mod")
    R_rope = Reg("rope")
    invc = sb("invc", [128, 4, 2, 16], F32)
    kb.dma(invc[:], invc_in[:, :, :, :], w=[R_rope])

    def rsqrt(out_ap, in_ap, const, r, w):
        kb.op("act", lambda e: e.activation(out=out_ap, in_=in_ap, func=AF.Sqrt, bias=float(const)), r=r, w=w)
        kb.op("dve", lambda e: e.reciprocal(out_ap, out_ap), r=w, w=w)

    def stage_scope():
        kb.full_barrier()
        return ExitStack()

    def mark(name):
        kb.mute = (C.stages is not None) and (name not in C.stages)

    class Rot:
        def __init__(self, st, name, shape, dt, n=2):
            self.t = [st.enter_context(nc.sbuf_tensor(U("%s%d" % (name, i)), list(shape), dt)) for i in range(n)]
            self.r = [Reg("%s%d" % (name, i)) for i in range(n)]
            self.i = 0

        def get(self):
            k = self.i % len(self.t)
            self.i += 1
            return self.t[k], self.r[k]

    def load_w(st, name, src_ap, shape, rot_stage=None, eng="pool"):
        wt = st.enter_context(nc.sbuf_tensor(U(name), list(shape), BF16))
        rw = Reg(name)
        n1 = shape[-1]
        P = shape[0]
        stg = rot_stage
        for c in range(shape[1]) if len(shape) == 3 else [None]:
            s_t, s_r = stg.get()
            if c is None:
                kb.dma(s_t[0:P, 0:n1], src_ap, w=[s_r])
                kb.op(eng, lambda e, s_t=s_t: e.tensor_copy(wt[:, :], s_t[0:P, 0:n1]), r=[s_r], w=[rw])
            else:
                kb.dma(s_t[0:P, 0:n1], src_ap[:, c, :], w=[s_r])
                kb.op(eng, lambda e, s_t=s_t, c=c: e.tensor_copy(wt[:, c, :], s_t[0:P, 0:n1]), r=[s_r], w=[rw])
        return wt, rw

    def wview(w, li, c0, n):
        return w[li, :, c0:c0 + n].rearrange("(c p) n -> p c n", p=128)

    mark("PRO")
    with stage_scope() as st:
        xin = Rot(st, "xin", [128, D], F32)
        xo = Rot(st, "xo", [128, DC, 128], F32)
        for ti in range(C.NTILE):
            src = ctx_in[ti * 128:(ti + 1) * 128, :] if ti < C.CT else x_in[(ti - C.CT) * 128:(ti - C.CT + 1) * 128, :]
            xt, xr = xin.get()
            kb.dma(xt[:], src, w=[xr])
            ot, orr = xo.get()
            for c4 in range(0, DC, 4):
                pst, psr = psum()
                for c in range(c4, min(c4 + 4, DC)):
                    kb.op("pe", lambda e, pst=pst, xt=xt, c=c, c4=c4: e.transpose(pst[:, (c - c4) * 128:(c - c4 + 1) * 128], xt[:, c * 128:(c + 1) * 128], identf), r=[xr, R_cst], w=[psr])
                nn = min(4, DC - c4)
                kb.op("act", lambda e, pst=pst, ot=ot, c4=c4, nn=nn: e.activation(out=ot[:, c4:c4 + nn, :], in_=pst[:, 0:nn * 128].rearrange("p (c t) -> p c t", t=128), func=AF.Copy), r=[psr], w=[orr])
            kb.dma(xd[:, :, ti * 128:(ti + 1) * 128], ot[:], r=[orr], w=[R_xd])

    def norm_block(st_rot, xt, xr, c0, n, seg, ia, ib):
        sq, sqr = st_rot["sq"].get()
        kb.op("act", lambda e: e.activation(out=sq[:, :, 0:n], in_=xt[:, :, 0:n], func=AF.Square), r=[xr], w=[sqr])
        pst, psr = psum()
        for c in range(DC):
            kb.op("pe", lambda e, c=c: e.matmul(pst[:, 0:n], onesb[:], sq[:, c, 0:n], start=(c == 0), stop=(c == DC - 1)), r=[sqr, R_cst], w=[psr])
        rs, rsr = st_rot["rs"].get()
        rsqrt(rs[:, 0:n], pst[:, 0:n], D * C.EPS, [psr], [rsr])
        tm, tmr = st_rot["tm"].get()
        kb.op("dve", lambda e: e.tensor_tensor(tm[:, :, 0:n], xt[:, :, 0:n], rs[:, 0:n].unsqueeze(1).broadcast_to([128, DC, n]), ALU.mult), r=[xr, rsr], w=[tmr])
        hb, hbr = st_rot["hb"].get()
        for c in range(DC):
            kb.op("pool" if c % 2 else "dve", lambda e, c=c: e.tensor_scalar(hb[:, c, 0:n], tm[:, c, 0:n], AB[:, ia, c, seg:seg + 1], AB[:, ib, c, seg:seg + 1], ALU.mult, ALU.add), r=[tmr, R_mod], w=[hbr])
        kb.dma(hd[:, :, C.ext(c0):C.ext(c0) + n], hb[:, :, 0:n], r=[hbr], w=[R_hd])

    def blocks(w):
        res = []
        for seg, s0, ln in C.segs():
            c = 0
            while c < ln:
                n = min(w, ln - c)
                res.append((seg, s0 + c, n, c == 0, c + n == ln))
                c += n
        return res

    for li in range(L):
        last = li == L - 1
        segs_out = [(0, CTX, T)] if last else C.segs()

        mark("MOD")
        with stage_scope() as st:
            kb.dma(pf[:], pf_in[li, :, :], w=[R_pf])
            kb.dma(pt[:], pt_in[li, :, :], w=[R_pf])
            wst = Rot(st, "wmst", [128, DC, 512], F32)
            pst, psr = psum()
            for cb in range(0, 6 * D, 512):
                wt, wr = wst.get()
                kb.dma(wt[:], w_mod[li, :, cb:cb + 512].rearrange("(c p) n -> p c n", p=128), w=[wr])
                for j in range(4):
                    cc = cb // 128 + j
                    for c in range(DC):
                        kb.op("pe", lambda e, wt=wt, j=j, c=c, cc=cc: e.matmul(pst[:, cc * 2:cc * 2 + 2], wt[:, c, j * 128:(j + 1) * 128], conds[:, c, :], start=(c == 0), stop=(c == DC - 1)), r=[wr, R_cond], w=[psr])
            bm_ = pf[:, 0:6 * DC]
            kb.op("dve", lambda e: e.tensor_tensor(modT[:], pst[:, 0:12 * DC].rearrange("p (c j) -> p c j", j=2), bm_.unsqueeze(2).broadcast_to([128, 6 * DC, 2]), ALU.add), r=[psr, R_pf], w=[R_mod])
            mv = lambda k: modT[:, k * DC:(k + 1) * DC, :]
            nmix = pf[:, 6 * DC:7 * DC].unsqueeze(2).broadcast_to([128, DC, 2])
            nffn = pf[:, 7 * DC:8 * DC].unsqueeze(2).broadcast_to([128, DC, 2])
            sD = float(math.sqrt(D))
            for (ia, ksc, ksh, kg, nrm) in ((0, 1, 0, 2, nmix), (3, 4, 3, 5, nffn)):
                kb.op("dve", lambda e, ia=ia, ksc=ksc: e.tensor_scalar(AB[:, ia], mv(ksc), 1.0, sD, ALU.add, ALU.mult), r=[R_mod], w=[R_mod])
                kb.op("dve", lambda e, ia=ia, nrm=nrm: e.tensor_tensor(AB[:, ia], AB[:, ia], nrm, ALU.mult), r=[R_mod, R_pf], w=[R_mod])
                kb.op("dve", lambda e, ia=ia, ksh=ksh: e.tensor_copy(AB[:, ia + 1], mv(ksh)), r=[R_mod], w=[R_mod])
                kb.op("dve", lambda e, ia=ia, kg=kg: e.tensor_copy(AB[:, ia + 2], mv(kg)), r=[R_mod], w=[R_mod])
            kb.op("act", lambda e: e.activation(out=pt[:, 1168:1184], in_=pt[:, 1168:1184], func=AF.Exp), r=[R_pf], w=[R_pf])
            kb.op("dve", lambda e: e.tensor_scalar(pt[:, 1168:1184], pt[:, 1168:1184], -1.0, None, ALU.mult), r=[R_pf], w=[R_pf])
            kb.op("dve", lambda e: e.tensor_scalar(pt[:, 640:1152], pt[:, 640:1152], float(math.sqrt(512.0)), None, ALU.mult), r=[R_pf], w=[R_pf])

        P_CS = 8 * DC
        P_PS = P_CS + 12
        P_SW = P_PS + 4
        P_SB = P_SW + 18
        P_FC = P_SB + 6
        qg, kg_, g512, dtb, arate, dsk = pt[:, 0:512], pt[:, 512:640], pt[:, 640:1152], pt[:, 1152:1168], pt[:, 1168:1184], pt[:, 1184:1192]

        mark("SA")
        def norm_rots(st, W):
            return dict(sq=Rot(st, "sq", [128, DC, W], BF16, 1), rs=Rot(st, "rs", [128, W], F32, 1),
                        tm=Rot(st, "tm", [128, DC, W], F32, 1), hb=Rot(st, "hb", [128, DC, W], BF16, 2))

        with stage_scope() as st:
            rots = norm_rots(st, 512)
            xb = Rot(st, "xb", [128, DC, 512], F32)
            for seg, c0, n, f, l_ in blocks(512):
                xt, xr = xb.get()
                kb.dma(xt[:, :, 0:n], xd[:, :, c0:c0 + n], r=[R_xd], w=[xr])
                norm_block(rots, xt, xr, c0, n, seg, 0, 1)

        mark("SB")
        stB = stage_scope()
        KT = stB.enter_context(nc.sbuf_tensor(U("KT"), [128, NT], BF16))
        VF = stB.enter_context(nc.sbuf_tensor(U("VF"), [128, C.NTILE, 2, 65], BF16))
        R_KT, R_VF = Reg("KT"), Reg("VF")
        kb.op("dve", lambda e: e.memset(VF[:].rearrange("p a b c -> p (a b c)"), 1.0), w=[R_VF])
        rope = stB.enter_context(nc.sbuf_tensor(U("rope"), [128, C.LT, 64], F32))
        kb.dma(rope[:], rope_in[:, :, :], w=[R_rope])
        with ExitStack() as st:
            stg = Rot(st, "wstg", [128, 768], F32)
            wq, wqr = load_w(st, "wqkv", wview(w_in, li, 0, 768), [128, DC, 768], stg)
            hb = Rot(st, "hbq", [128, DC, 128], BF16)
            t32 = Rot(st, "t32", [128, 640], F32, 2)
            t32b = Rot(st, "t32b", [128, 640], F32, 2)
            ssr = Rot(st, "ssr", [128, 16], F32, 2)
            qkb = Rot(st, "qkb", [128, 640], BF16, 2)
            qk2 = Rot(st, "qk2", [128, 512], BF16, 2)
            qT = Rot(st, "qT", [128, 4, 128], BF16, 2)
            for ti in range(C.NTILE):
                isctx = ti < C.CT
                c0 = ti * 128
                ht, hr = hb.get()
                kb.dma(ht[:], hd[:, :, C.ext(c0):C.ext(c0) + 128], r=[R_hd], w=[hr])
                pq, pqr = psum()
                pk, pkr = psum()
                for c in range(DC):
                    kb.op("pe", lambda e, c=c, ht=ht, pq=pq: e.matmul(pq[:, 0:512], ht[:, c, :], wq[:, c, 0:512], start=(c == 0), stop=(c == DC - 1)), r=[hr, wqr], w=[pqr])
                for c in range(DC):
                    kb.op("pe", lambda e, c=c, ht=ht, pk=pk: e.matmul(pk[:, 0:256], ht[:, c, :], wq[:, c, 512:768], start=(c == 0), stop=(c == DC - 1)), r=[hr, wqr], w=[pkr])
                kb.op("act", lambda e, pk=pk, ti=ti: e.activation(out=VF[:, ti, :, 0:64], in_=pk[:, 128:256].rearrange("p (g d) -> p g d", d=64), func=AF.Copy), r=[pkr], w=[R_VF])
                need_q = (not isctx) or (not last)
                if C.sbl < 2:
                    continue
                a, ar = t32.get()
                b, br = t32b.get()
                s, sr = ssr.get()
                qb, qbr = qkb.get()
                if need_q:
                    kb.op("act", lambda e, a=a, pq=pq: e.activation(out=a[:, 0:512], in_=pq[:, 0:512], func=AF.Copy), r=[pqr], w=[ar])
                kb.op("act", lambda e, a=a, pk=pk: e.activation(out=a[:, 512:640], in_=pk[:, 0:128], func=AF.Copy), r=[pkr], w=[ar])
                lo = 0 if need_q else 512
                nh = (640 - lo) // 64
                kb.op("dve", lambda e, a=a, b=b, lo=lo: e.tensor_tensor(b[:, lo:640], a[:, lo:640], a[:, lo:640], ALU.mult), r=[ar], w=[br])
                kb.op("dve", lambda e, s=s, b=b, lo=lo, nh=nh: e.tensor_reduce(s[:, 0:nh], b[:, lo:640].rearrange("p (h d) -> p h d", d=64), AX.X, ALU.add), r=[br], w=[sr])
                rsqrt(s[:, 0:nh], s[:, 0:nh], 64 * C.EPS, [sr], [sr])
                kb.op("dve", lambda e, a=a, s=s, lo=lo, nh=nh: e.tensor_tensor(a[:, lo:640].rearrange("p (h d) -> p h d", d=64), a[:, lo:640].rearrange("p (h d) -> p h d", d=64), s[:, 0:nh].unsqueeze(2).broadcast_to([128, nh, 64]), ALU.mult), r=[ar, sr], w=[ar])
                kb.op("dve", lambda e, a=a, lo=lo: e.tensor_tensor(a[:, lo:640], a[:, lo:640], pt[:, lo:640], ALU.mult), r=[ar, R_pf], w=[ar])
                if C.sbl < 3:
                    continue
                if isctx:
                    kb.op("dve", lambda e, a=a, qb=qb, lo=lo: e.tensor_copy(qb[:, lo:640], a[:, lo:640]), r=[ar], w=[qbr])
                else:
                    li_ = ti - C.CT
                    av = a[:, 0:640].rearrange("p (h x q f) -> p h x q f", x=2, q=2, f=16)
                    bv = b[:, 0:640].rearrange("p (h x q f) -> p h x q f", x=2, q=2, f=16)
                    qv = qb[:, 0:640].rearrange("p (h x q f) -> p h x q f", x=2, q=2, f=16)
                    for x in range(2):
                        cs = rope[:, li_, x * 16:(x + 1) * 16].unsqueeze(1).broadcast_to([128, 10, 16])
                        sn = rope[:, li_, 32 + x * 16:32 + (x + 1) * 16].unsqueeze(1).broadcast_to([128, 10, 16])
                        t1, t2 = av[:, :, x, 0, :], av[:, :, x, 1, :]
                        b1, b2 = bv[:, :, x, 0, :], bv[:, :, x, 1, :]
                        kb.op("dve", lambda e, b1=b1, t1=t1, cs=cs: e.tensor_tensor(b1, t1, cs, ALU.mult), r=[ar, R_rope], w=[br])
                        kb.op("dve", lambda e, b2=b2, t2=t2, sn=sn: e.tensor_tensor(b2, t2, sn, ALU.mult), r=[ar, R_rope], w=[br])
                        kb.op("dve", lambda e, b1=b1, b2=b2, qv=qv, x=x: e.tensor_tensor(qv[:, :, x, 0, :], b1, b2, ALU.subtract), r=[br], w=[qbr])
                        kb.op("dve", lambda e, b1=b1, t1=t1, sn=sn: e.tensor_tensor(b1, t1, sn, ALU.mult), r=[ar, R_rope, qbr], w=[br])
                        kb.op("dve", lambda e, b2=b2, t2=t2, cs=cs: e.tensor_tensor(b2, t2, cs, ALU.mult), r=[ar, R_rope, qbr], w=[br])
                        kb.op("dve", lambda e, b1=b1, b2=b2, qv=qv, x=x: e.tensor_tensor(qv[:, :, x, 1, :], b1, b2, ALU.add), r=[br], w=[qbr])
                if C.sbl < 4:
                    continue
                ptp, ptr = psum()
                pb = ptp[:, :].bitcast(BF16)
                kb.op("pe", lambda e, pb=pb, qb=qb: e.transpose(pb[:, 0:128], qb[:, 512:640], identb[:]), r=[qbr, R_cst], w=[ptr])
                if need_q and C.sbl >= 5:
                    q2, q2r = qk2.get()
                    for g in range(2):
                        kb.op("dve", lambda e, q2=q2, qb=qb, g=g: e.tensor_copy(q2[:].rearrange("p (r g d) -> p r g d", g=2, r=4)[:, :, g, :], qb[:, g * 256:(g + 1) * 256].rearrange("p (r d) -> p r d", d=64)), r=[qbr], w=[q2r])
                    for j in range(4):
                        kb.op("pe", lambda e, pb=pb, j=j, q2=q2: e.transpose(pb[:, 128 * (j + 1):128 * (j + 2)], q2[:, j * 128:(j + 1) * 128], identb[:]), r=[q2r, R_cst], w=[ptr])
                kb.op("act", lambda e, pb=pb, c0=c0: e.activation(out=KT[:, c0:c0 + 128], in_=pb[:, 0:128], func=AF.Copy), r=[ptr], w=[R_KT])
                if need_q and C.sbl >= 6:
                    qt_, qtr = qT.get()
                    kb.op("act", lambda e, pb=pb, qt_=qt_: e.activation(out=qt_[:], in_=pb[:, 128:640].rearrange("p (j t) -> p j t", t=128), func=AF.Copy), r=[ptr], w=[qtr])
                    kb.dma(qd[:, :, c0:c0 + 128], qt_[:], r=[qtr], w=[R_qd])

        mark("SC")
        with ExitStack() as st:
            kb.full_barrier()
            qbk = Rot(st, "qbk", [128, 4, 512], BF16)
            pT = Rot(st, "pT", [128, 512], BF16, 4)
            otm = Rot(st, "otm", [65, 512], F32, 2)
            rcp = Rot(st, "rcp", [65, 512], F32, 2)
            oTn = Rot(st, "oTn", [64, 512], BF16, 2)
            ones65 = st.enter_context(nc.sbuf_tensor(U("ones65"), [65, 64], F32))
            kb.op("pool", lambda e: e.memset(ones65[:], 1.0), w=[R_cst])
            psO, psS, psB = PsRot([0, 1]), PsRot([2, 3, 4, 5]), PsRot([6, 7])
            qblocks = []
            if not last:
                qblocks.append((0, CTX, list(range(C.CT))))
            for c in range(CTX, NT, 512):
                qblocks.append((c, min(512, NT - c), list(range(C.NTILE))))
            for (c0, n, ktiles) in qblocks:
                qt_, qtr = qbk.get()
                kb.dma(qt_[:, :, 0:n], qd[:, :, c0:c0 + n], r=[R_qd], w=[qtr])
                for j in range(4):
                    po = [psO.get(), psO.get()]
                    for ki, kt in enumerate(ktiles):
                        for g in range(2):
                            pS, pSr = psS.get()
                            rows = slice(g * 64, g * 64 + 64)
                            kb.op("pe", lambda e, pS=pS, rows=rows, kt=kt, qt_=qt_, j=j: e.matmul(pS[:, 0:n], KT[rows, kt * 128:(kt + 1) * 128], qt_[rows, j, 0:n], start=True, stop=True), r=[R_KT, qtr], w=[pSr])
                            p_, pr_ = pT.get()
                            kb.op("act", lambda e, p_=p_, pS=pS: e.activation(out=p_[:, 0:n], in_=pS[:, 0:n], func=AF.Exp, scale=8.0), r=[pSr], w=[pr_])
                            kb.op("pe", lambda e, g=g, kt=kt, p_=p_, po=po, ki=ki: e.matmul(po[g][0][0:65, 0:n], VF[:, kt, g, :], p_[:, 0:n], start=(ki == 0), stop=(ki == len(ktiles) - 1)), r=[R_VF, pr_], w=[po[g][1]])
                    for g in range(2):
                        h = j + 4 * g
                        o_, or_ = otm.get()
                        kb.op("act", lambda e, o_=o_, g=g, po=po: e.activation(out=o_[:, 0:n], in_=po[g][0][0:65, 0:n], func=AF.Copy), r=[po[g][1]], w=[or_])
                        r_, rr_ = rcp.get()
                        kb.op("dve", lambda e, o_=o_: e.reciprocal(o_[64:65, 0:n], o_[64:65, 0:n]), r=[or_], w=[or_])
                        kb.op("dve", lambda e, o_=o_, r_=r_: e.tensor_copy(r_[64:65, 0:n], o_[64:65, 0:n]), r=[or_], w=[rr_])
                        pB, pBr = psB.get()
                        kb.op("pe", lambda e, pB=pB, r_=r_: e.matmul(pB[0:64, 0:n], ones65[64:65, :], r_[64:65, 0:n], start=True, stop=True), r=[rr_, R_cst], w=[pBr])
                        on_, onr = oTn.get()
                        kb.op("dve", lambda e, on_=on_, o_=o_, pB=pB: e.tensor_tensor(on_[:, 0:n], o_[0:64, 0:n], pB[0:64, 0:n], ALU.mult), r=[or_, pBr], w=[onr])
                        kb.dma(attd[:, h, c0:c0 + n], on_[:, 0:n], r=[onr], w=[R_attd])
        stB.close()

        mark("SD")
        with stage_scope() as st:
            stg = Rot(st, "wstg", [128, 1536], F32)
            wc, wcr = load_w(st, "wconv", wview(w_in, li, OFF["bg"], 1536), [128, DC, 1536], stg)
            hbk = Rot(st, "hbk", [128, DC, 512], BF16)
            cgs = Rot(st, "cgs", [128, 512], F32, 2)
            cu = Rot(st, "cu", [128, 512], F32, 2)
            tt = Rot(st, "tt", [128, 512], F32, 2)
            ob = Rot(st, "ob", [128, 4, 512], BF16, 2)
            for seg, c0, n, f, l_ in blocks(510):
                if last and seg == 1:
                    continue
                ht, hr = hbk.get()
                kb.dma(ht[:, :, 0:n + 2], hd[:, :, C.ext(c0) - 1:C.ext(c0) + n + 1], r=[R_hd], w=[hr])
                o_, or_ = ob.get()
                for i in range(4):
                    pss = [psum() for _ in range(3)]
                    for k, nm in enumerate(("bg", "cg", "u")):
                        wc0 = (OFF[nm] - OFF["bg"]) + i * 128
                        for c in range(DC):
                            kb.op("pe", lambda e, k=k, c=c, wc0=wc0, ht=ht, pss=pss: e.matmul(pss[k][0][:, 0:n + 2], wc[:, c, wc0:wc0 + 128], ht[:, c, 0:n + 2], start=(c == 0), stop=(c == DC - 1)), r=[hr, wcr], w=[pss[k][1]])
                    g_, gr_ = cgs.get()
                    kb.op("act", lambda e, g_=g_, pss=pss: e.activation(out=g_[:, 0:n + 2], in_=pss[1][0][:, 0:n + 2], func=AF.Copy), r=[pss[1][1]], w=[gr_])
                    u_, ur_ = cu.get()
                    kb.op("dve", lambda e, u_=u_, g_=g_, pss=pss: e.tensor_tensor(u_[:, 0:n + 2], g_[:, 0:n + 2], pss[2][0][:, 0:n + 2], ALU.mult), r=[gr_, pss[2][1]], w=[ur_])
                    t_, tr_ = tt.get()
                    w3 = pf[:, P_CS + 3 * i:P_CS + 3 * i + 3]
                    kb.op("pool", lambda e, t_=t_, u_=u_, w3=w3: e.tensor_scalar(t_[:, 0:n], u_[:, 1:n + 1], w3[:, 1:2], None, ALU.mult), r=[ur_, R_pf], w=[tr_])
                    kb.op("dve", lambda e, t_=t_, u_=u_, w3=w3: e.scalar_tensor_tensor(t_[:, 0:n], u_[:, 0:n], w3[:, 0:1], t_[:, 0:n], ALU.mult, ALU.add), r=[ur_, R_pf, tr_], w=[tr_])
                    kb.op("dve", lambda e, t_=t_, u_=u_, w3=w3: e.scalar_tensor_tensor(t_[:, 0:n], u_[:, 2:n + 2], w3[:, 2:3], t_[:, 0:n], ALU.mult, ALU.add), r=[ur_, R_pf, tr_], w=[tr_])
                    kb.op("dve", lambda e, t_=t_, o_=o_, i=i, pss=pss: e.tensor_tensor(o_[:, i, 0:n], t_[:, 0:n], pss[0][0][:, 1:n + 1], ALU.mult), r=[tr_, pss[0][1]], w=[or_])
                kb.dma(bcd[:, :, c0:c0 + n], o_[:, :, 0:n], r=[or_], w=[R_bcd])

        mark("SE")
        with stage_scope() as st:
            stg = Rot(st, "wstg", [128, 512], F32)
            wp, wpr = load_w(st, "wpu", wview(w_in, li, OFF["pu"], 512), [128, DC, 512], stg)
            wpl, wplr = load_w(st, "wpool", pool_w[li].rearrange("g c d -> c g d"), [128, 4, 128], stg)
            hbk = Rot(st, "hbk", [128, DC, 512], BF16)
            u32 = Rot(st, "u32", [128, 512], F32, 2)
            sA = Rot(st, "sA", [128, 512], F32, 2)
            sB = Rot(st, "sB", [128, 512], F32, 2)
            pl = Rot(st, "pl", [128, 512], F32, 2)
            plb = Rot(st, "plb", [128, 512], BF16, 2)
            ob = Rot(st, "ob", [128, 4, 512], BF16, 2)
            for seg, c0, n, f, l_ in blocks(496):
                if last and seg == 1:
                    continue
                ht, hr = hbk.get()
                kb.dma(ht[:, :, 0:n + 16], hd[:, :, C.ext(c0) - 8:C.ext(c0) + n + 8], r=[R_hd], w=[hr])
                o_, or_ = ob.get()
                m = n + 16
                for g in range(4):
                    pu_, pur = psum()
                    for c in range(DC):
                        kb.op("pe", lambda e, c=c, g=g, ht=ht, pu_=pu_: e.matmul(pu_[:, 0:m], wp[:, c, g * 128:(g + 1) * 128], ht[:, c, 0:m], start=(c == 0), stop=(c == DC - 1)), r=[hr, wpr], w=[pur])
                    u_, ur_ = u32.get()
                    kb.op("act", lambda e, u_=u_, pu_=pu_: e.activation(out=u_[:, 0:m], in_=pu_[:, 0:m], func=AF.Copy), r=[pur], w=[ur_])
                    a_, ar_ = sA.get()
                    b_, br_ = sB.get()
                    kb.op("dve", lambda e, a_=a_, u_=u_: e.tensor_tensor(a_[:, 1:m], u_[:, 0:m - 1], u_[:, 1:m], ALU.add), r=[ur_], w=[ar_])
                    cur, curr, oth, othr = a_, ar_, b_, br_
                    lo, hi, sh = 1, m, 1
                    for step in range(g):
                        kb.op("dve", lambda e, cur=cur, oth=oth, lo=lo, hi=hi, sh=sh: e.tensor_tensor(oth[:, lo + sh:hi - sh], cur[:, lo:hi - 2 * sh], cur[:, lo + 2 * sh:hi], ALU.add), r=[curr], w=[othr])
                        lo, hi, sh = lo + sh, hi - sh, sh * 2
                        cur, curr, oth, othr = oth, othr, cur, curr
                    wnd = 2 ** (g + 1)
                    p_, pr_ = pl.get()
                    kb.op("dve", lambda e, p_=p_, cur=cur, u_=u_, wnd=wnd: e.scalar_tensor_tensor(p_[:, 0:n], cur[:, 8:8 + n], 1.0 / wnd, u_[:, 8:8 + n], ALU.mult, ALU.subtract), r=[curr, ur_], w=[pr_])
                    for (flag, cc, tc) in ((f, 0, 0), (l_, n - 8, 8)):
                        if flag:
                            kb.op("dve", lambda e, oth=oth, cur=cur, cc=cc, tc=tc, g=g, seg=seg: e.tensor_tensor(oth[:, 0:8], cur[:, 8 + cc:16 + cc], invc[:, g, seg, tc:tc + 8], ALU.mult), r=[curr, R_rope], w=[othr])
                            kb.op("dve", lambda e, oth=oth, p_=p_, u_=u_, cc=cc: e.tensor_tensor(p_[:, cc:cc + 8], oth[:, 0:8], u_[:, 8 + cc:16 + cc], ALU.subtract), r=[othr, ur_, pr_], w=[pr_])
                    pb_, pbr = plb.get()
                    kb.op("act", lambda e, pb_=pb_, p_=p_: e.activation(out=pb_[:, 0:n], in_=p_[:, 0:n], func=AF.Copy), r=[pr_], w=[pbr])
                    po_, por = psum()
                    kb.op("pe", lambda e, po_=po_, g=g, pb_=pb_: e.matmul(po_[:, 0:n], wpl[:, g, :], pb_[:, 0:n], start=True, stop=True), r=[wplr, pbr], w=[por])
                    kb.op("dve", lambda e, o_=o_, g=g, po_=po_: e.tensor_scalar(o_[:, g, 0:n], po_[:, 0:n], pf[:, P_PS + g:P_PS + g + 1], None, ALU.mult), r=[por, R_pf], w=[or_])
                kb.dma(bpd[:, :, c0:c0 + n], o_[:, :, 0:n], r=[or_], w=[R_bpd])

        mark("SF")
        with stage_scope() as st:
            stg = Rot(st, "wstg", [128, 1296], F32)
            ws, wsr = load_w(st, "wssd", wview(w_in, li, OFF["z"], 1296), [128, DC, 1296], stg)
            ZO, XO, DO = 0, 512, 1280
            hbk = Rot(st, "hbk", [128, DC, 386], BF16)
            cv = Rot(st, "cv", [128, 384], F32, 2)
            ac = Rot(st, "ac", [128, 384], BF16, 2)
            xsb = Rot(st, "xsb", [128, 3, 512], BF16, 2)
            bmb = Rot(st, "bmb", [128, 3, 128], BF16, 2)
            zsb = Rot(st, "zsb", [128, 3, 512], BF16, 2)
            dtb_ = Rot(st, "dtb", [128, 3, 32], F32, 2)
            d1 = Rot(st, "d1", [128, 16], F32, 2)
            d2 = Rot(st, "d2", [128, 16], F32, 2)
            for seg, c0, n, f, l_ in blocks(384):
                nt_ = n // 128
                t0 = c0 // 128
                ht, hr = hbk.get()
                kb.dma(ht[:, :, 0:n + 2], hd[:, :, C.ext(c0) - 1:C.ext(c0) + n + 1], r=[R_hd], w=[hr])
                xs_, xsr = xsb.get()
                bm_, bmr = bmb.get()
                for i in range(6):
                    px, pxr = psum()
                    for c in range(DC):
                        kb.op("pe", lambda e, c=c, i=i, px=px, ht=ht: e.matmul(px[:, 0:n + 2], ws[:, c, XO + i * 128:XO + (i + 1) * 128], ht[:, c, 0:n + 2], start=(c == 0), stop=(c == DC - 1)), r=[hr, wsr], w=[pxr])
                    t_, tr_ = cv.get()
                    w3 = pf[:, P_SW + 3 * i:P_SW + 3 * i + 3]
                    bb = pf[:, P_SB + i:P_SB + i + 1]
                    kb.op("dve", lambda e, t_=t_, px=px, w3=w3, bb=bb: e.tensor_scalar(t_[:, 0:n], px[:, 1:n + 1], w3[:, 1:2], bb, ALU.mult, ALU.add), r=[pxr, R_pf], w=[tr_])
                    kb.op("dve", lambda e, t_=t_, px=px, w3=w3: e.scalar_tensor_tensor(t_[:, 0:n], px[:, 0:n], w3[:, 0:1], t_[:, 0:n], ALU.mult, ALU.add), r=[pxr, R_pf, tr_], w=[tr_])
                    kb.op("dve", lambda e, t_=t_, px=px, w3=w3: e.scalar_tensor_tensor(t_[:, 0:n], px[:, 2:n + 2], w3[:, 2:3], t_[:, 0:n], ALU.mult, ALU.add), r=[pxr, R_pf, tr_], w=[tr_])
                    a_, ar_ = ac.get()
                    kb.op("act", lambda e, a_=a_, t_=t_: e.activation(out=a_[:, 0:n], in_=t_[:, 0:n], func=AF.Silu), r=[tr_], w=[ar_])
                    if i >= 4:
                        kb.dma(bcT[:, i - 4, c0:c0 + n], a_[:, 0:n], r=[ar_], w=[R_ssd])
                    if i <= 4:
                        ptp, ptr = psum()
                        pb = ptp[:, :].bitcast(BF16)
                        for k in range(nt_):
                            kb.op("pe", lambda e, pb=pb, k=k, a_=a_: e.transpose(pb[:, k * 128:(k + 1) * 128], a_[:, k * 128:(k + 1) * 128], identb[:]), r=[ar_, R_cst], w=[ptr])
                        if i < 4:
                            kb.op("act", lambda e, xs_=xs_, pb=pb, i=i: e.activation(out=xs_[:, 0:nt_, i * 128:(i + 1) * 128], in_=pb[:, 0:nt_ * 128].rearrange("p (k f) -> p k f", f=128), func=AF.Copy), r=[ptr], w=[xsr])
                        else:
                            kb.op("act", lambda e, bm_=bm_, pb=pb: e.activation(out=bm_[:, 0:nt_, :], in_=pb[:, 0:nt_ * 128].rearrange("p (k f) -> p k f", f=128), func=AF.Copy), r=[ptr], w=[bmr])
                kb.dma(xsd[:, t0:t0 + nt_, :], xs_[:, 0:nt_, :], r=[xsr], w=[R_ssd])
                kb.dma(bmd[:, t0:t0 + nt_, :], bm_[:, 0:nt_, :], r=[bmr], w=[R_ssd])
                z_, zr_ = zsb.get()
                dd, ddr = dtb_.get()
                for k in range(nt_):
                    pz, pzr = psum()
                    pd, pdr = psum()
                    for c in range(DC):
                        kb.op("pe", lambda e, c=c, k=k, pz=pz, ht=ht: e.matmul(pz[:, 0:512], ht[:, c, 1 + k * 128:1 + (k + 1) * 128], ws[:, c, ZO:ZO + 512], start=(c == 0), stop=(c == DC - 1)), r=[hr, wsr], w=[pzr])
                    for c in range(DC):
                        kb.op("pe", lambda e, c=c, k=k, pd=pd, ht=ht: e.matmul(pd[:, 0:16], ht[:, c, 1 + k * 128:1 + (k + 1) * 128], ws[:, c, DO:DO + 16], start=(c == 0), stop=(c == DC - 1)), r=[hr, wsr], w=[pdr])
                    kb.op("act", lambda e, z_=z_, k=k, pz=pz: e.activation(out=z_[:, k, :], in_=pz[:, 0:512], func=AF.Silu), r=[pzr], w=[zr_])
                    x1, x1r = d1.get()
                    x2, x2r = d2.get()
                    kb.op("dve", lambda e, x1=x1, pd=pd: e.tensor_tensor(x1[:], pd[:, 0:16], dtb, ALU.add), r=[pdr, R_pf], w=[x1r])
                    kb.op("act", lambda e, x1=x1, x2=x2: e.activation(out=x2[:], in_=x1[:], func=AF.Abs), r=[x1r], w=[x2r])
                    kb.op("act", lambda e, x2=x2: e.activation(out=x2[:], in_=x2[:], func=AF.Exp, scale=-1.0), r=[x2r], w=[x2r])
                    kb.op("act", lambda e, x2=x2: e.activation(out=x2[:], in_=x2[:], func=AF.Ln, bias=1.0), r=[x2r], w=[x2r])
                    kb.op("dve", lambda e, dd=dd, k=k, x1=x1, x2=x2: e.scalar_tensor_tensor(dd[:, k, 0:16], x1[:], 0.0, x2[:], ALU.max, ALU.add), r=[x1r, x2r], w=[ddr])
                    kb.op("dve", lambda e, dd=dd, k=k: e.tensor_tensor(dd[:, k, 16:32], dd[:, k, 0:16], arate, ALU.mult), r=[ddr, R_pf], w=[ddr])
                kb.dma(zsd[:, t0:t0 + nt_, :], z_[:, 0:nt_, :], r=[zr_], w=[R_ssd])
                kb.dma(dtd[:, t0:t0 + nt_, :], dd[:, 0:nt_, :], r=[ddr], w=[R_ssd])

        mark("SG")
        with stage_scope() as st:
            hs = st.enter_context(nc.sbuf_tensor(U("hs"), [128, 256], F32))
            hsb = st.enter_context(nc.sbuf_tensor(U("hsb"), [128, 256], BF16))
            R_hs, R_hsb = Reg("hs"), Reg("hsb")
            xsl = Rot(st, "xsl", [128, 512], BF16, 2)
            bml = Rot(st, "bml", [128, 128], BF16, 2)
            bcl = Rot(st, "bcl", [128, 2, 128], BF16, 2)
            dtl = Rot(st, "dtl", [128, 32], F32, 2)
            zl = Rot(st, "zl", [128, 512], BF16, 2)
            yl = Rot(st, "yl", [128, 512], F32, 2)
            rA = Rot(st, "rA", [128, 1024], F32, 2)
            LTt = Rot(st, "LTt", [128, 8, 128], F32, 2)
            WTt = Rot(st, "WTt", [128, 8, 128], BF16, 2)
            dA = Rot(st, "dA", [128, 16], F32, 2)
            xdt = Rot(st, "xdt", [128, 512], BF16, 2)
            x32t = Rot(st, "x32t", [128, 512], F32, 2)
            xd32t = Rot(st, "xd32t", [128, 512], F32, 2)
            z32t = Rot(st, "z32t", [128, 512], F32, 2)
            cbs = Rot(st, "cbs", [128, 256], F32, 2)
            xdc = Rot(st, "xdc", [128, 512], BF16, 2)
            yt = Rot(st, "yt", [128, 512], F32, 2)
            yo = Rot(st, "yo", [128, 512], F32, 2)
            sq5 = Rot(st, "sq5", [128, 512], F32, 1)
            s1 = Rot(st, "s1", [128, 1], F32, 2)
            ynb = Rot(st, "ynb", [128, 512], BF16, 2)
            yT = Rot(st, "yT", [128, 4, 128], BF16, 2)
            for d in range(2):
                M1, M2, MK = cst[:, 1 + 3 * d, :], cst[:, 2 + 3 * d, :], cst[:, 3 + 3 * d, :]
                dcol = 127 if d == 0 else 0
                order = list(range(C.CT)) + list(range(C.CT, C.NTILE))
                if d == 1:
                    order = list(range(C.CT - 1, -1, -1)) + list(range(C.NTILE - 1, C.CT - 1, -1))
                kb.op("dve", lambda e: e.memset(hs[:], 0.0), w=[R_hs])
                for ti in order:
                    isctx = ti < C.CT
                    need_y = not (isctx and last)
                    c0 = ti * 128
                    x_, xr_ = xsl.get()
                    kb.dma(x_[:], xsd[:, ti, :], r=[R_ssd], w=[xr_])
                    bm_, bmr = bml.get()
                    kb.dma(bm_[:], bmd[:, ti, :], r=[R_ssd], w=[bmr])
                    bc_, bcr = bcl.get()
                    kb.dma(bc_[:], bcT[:, :, c0:c0 + 128], r=[R_ssd], w=[bcr])
                    dt_, dtr = dtl.get()
                    kb.dma(dt_[:], dtd[:, ti, :], r=[R_ssd], w=[dtr])
                    dtv = dt_[:, d * 8:d * 8 + 8]
                    av = dt_[:, 16 + d * 8:16 + d * 8 + 8]
                    ra, rar = rA.get()
                    kb.op("dve", lambda e, ra=ra, av=av, M2=M2: e.tensor_tensor(ra[:].rearrange("p (h l) -> p h l", l=128), M2.unsqueeze(1).broadcast_to([128, 8, 128]), av.unsqueeze(2).broadcast_to([128, 8, 128]), ALU.mult), r=[dtr, R_cst], w=[rar])
                    pl0, pl0r = psum()
                    pl1, pl1r = psum()
                    kb.op("pe", lambda e, pl0=pl0, ra=ra, M1=M1: e.matmul(pl0[:, :], M1, ra[:, 0:512], start=True, stop=True), r=[rar, R_cst], w=[pl0r])
                    kb.op("pe", lambda e, pl1=pl1, ra=ra, M1=M1: e.matmul(pl1[:, :], M1, ra[:, 512:1024], start=True, stop=True), r=[rar, R_cst], w=[pl1r])
                    pa, par = psum()
                    kb.op("pe", lambda e, pa=pa, av=av, M2=M2: e.matmul(pa[:, 0:8], M2, av, start=True, stop=True), r=[dtr, R_cst], w=[par])
                    kb.op("pe", lambda e, pa=pa, av=av: e.matmul(pa[:, 8:16], onesf[:], av, start=True, stop=True), r=[dtr, R_cst], w=[par])
                    lt_, ltr = LTt.get()
                    kb.op("act", lambda e, lt_=lt_, pl0=pl0: e.activation(out=lt_[:, 0:4, :].rearrange("p h l -> p (h l)"), in_=pl0[:, :], func=AF.Exp), r=[pl0r], w=[ltr])
                    kb.op("act", lambda e, lt_=lt_, pl1=pl1: e.activation(out=lt_[:, 4:8, :].rearrange("p h l -> p (h l)"), in_=pl1[:, :], func=AF.Exp), r=[pl1r], w=[ltr])
                    da_, dar = dA.get()
                    kb.op("act", lambda e, da_=da_, pa=pa: e.activation(out=da_[:], in_=pa[:, 0:16], func=AF.Exp), r=[par], w=[dar])
                    if C.sgl < 2:
                        continue
                    x32, x32r = x32t.get()
                    kb.op("act", lambda e, x32=x32, x_=x_: e.activation(out=x32[:], in_=x_[:], func=AF.Copy), r=[xr_], w=[x32r])
                    xd32, xd32r = xd32t.get()
                    kb.op("dve", lambda e, xd32=xd32, x32=x32, dtv=dtv: e.tensor_tensor(xd32[:].rearrange("p (h q) -> p h q", q=64), x32[:].rearrange("p (h q) -> p h q", q=64), dtv.unsqueeze(2).broadcast_to([128, 8, 64]), ALU.mult), r=[x32r, dtr], w=[xd32r])
                    xd_, xdr = xdt.get()
                    kb.op("act", lambda e, xd_=xd_, xd32=xd32: e.activation(out=xd_[:], in_=xd32[:], func=AF.Copy), r=[xd32r], w=[xdr])
                    if need_y:
                        pcs = [psum(), psum()]
                        for g in range(2):
                            rows = slice(g * 64, g * 64 + 64)
                            kb.op("pe", lambda e, pcs=pcs, g=g, rows=rows, bc_=bc_: e.matmul(pcs[g][0][:, 0:128], bc_[rows, 0, :], bc_[rows, 1, :], start=True, stop=True), r=[bcr], w=[pcs[g][1]])
                        kb.op("dve", lambda e, lt_=lt_, MK=MK: e.tensor_tensor(lt_[:], lt_[:], MK.unsqueeze(1).broadcast_to([128, 8, 128]), ALU.mult), r=[ltr, R_cst], w=[ltr])
                        wt_, wtr = WTt.get()
                        cb_, cbr = cbs.get()
                        for g in range(2):
                            kb.op("act", lambda e, cb_=cb_, pcs=pcs, g=g: e.activation(out=cb_[:, g * 128:(g + 1) * 128], in_=pcs[g][0][:, 0:128], func=AF.Copy), r=[pcs[g][1]], w=[cbr])
                        for g in range(2):
                            kb.op("dve", lambda e, wt_=wt_, lt_=lt_, cb_=cb_, g=g: e.tensor_tensor(wt_[:, 4 * g:4 * g + 4, :], lt_[:, 4 * g:4 * g + 4, :], cb_[:, g * 128:(g + 1) * 128].unsqueeze(1).broadcast_to([128, 4, 128]), ALU.mult), r=[ltr, cbr], w=[wtr])
                        if C.sgl < 3:
                            continue
                        py, pyr = psum()
                        for h in range(8):
                            kb.op("pe", lambda e, py=py, h=h, wt_=wt_, xd_=xd_: e.matmul(py[:, h * 64:(h + 1) * 64], wt_[:, h, :], xd_[:, h * 64:(h + 1) * 64], start=True, stop=True), r=[wtr, xdr], w=[pyr])
                        kb.op("act", lambda e: e.activation(out=hsb[:], in_=hs[:], func=AF.Copy), r=[R_hs], w=[R_hsb])
                        pofs = [psum(), psum()]
                        for g in range(2):
                            rows = slice(g * 64, g * 64 + 64)
                            kb.op("pe", lambda e, pofs=pofs, g=g, rows=rows, bc_=bc_: e.matmul(pofs[g][0][:, 0:256], bc_[rows, 1, :], hsb[rows, :], start=True, stop=True), r=[bcr, R_hsb], w=[pofs[g][1]])
                        yt_, ytr = yt.get()
                        for g in range(2):
                            kb.op("dve", lambda e, yt_=yt_, pofs=pofs, da_=da_, g=g: e.tensor_tensor(yt_[:, g * 256:(g + 1) * 256].rearrange("p (h q) -> p h q", q=64), pofs[g][0][:, 0:256].rearrange("p (h q) -> p h q", q=64), da_[:, 4 * g:4 * g + 4].unsqueeze(2).broadcast_to([128, 4, 64]), ALU.mult), r=[pofs[g][1], dar], w=[ytr])
                        yo_, yor = yo.get()
                        kb.op("dve", lambda e, yo_=yo_, yt_=yt_, py=py: e.tensor_tensor(yo_[:], yt_[:], py[:, :], ALU.add), r=[ytr, pyr], w=[yor])
                        if d == 0:
                            kb.op("dve", lambda e, yt_=yt_, x32=x32: e.tensor_tensor(yt_[:].rearrange("p (h q) -> p h q", q=64), x32[:].rearrange("p (h q) -> p h q", q=64), dsk.unsqueeze(2).broadcast_to([128, 8, 64]), ALU.mult), r=[x32r, R_pf, yor], w=[ytr])
                            kb.op("pool", lambda e, yo_=yo_, yt_=yt_: e.tensor_tensor(yo_[:], yo_[:], yt_[:], ALU.add), r=[ytr, yor], w=[yor])
                            kb.dma(yd[:, ti, :], yo_[:], r=[yor], w=[R_yd])
                        else:
                            y_, yr_ = yl.get()
                            kb.dma(y_[:], yd[:, ti, :], r=[R_yd], w=[yr_])
                            z_, zr_ = zl.get()
                            kb.dma(z_[:], zsd[:, ti, :], r=[R_ssd], w=[zr_])
                            kb.op("pool", lambda e, yo_=yo_, y_=y_: e.tensor_tensor(yo_[:], yo_[:], y_[:], ALU.add), r=[yr_, yor], w=[yor])
                            z32, z32r = z32t.get()
                            kb.op("act", lambda e, z32=z32, z_=z_: e.activation(out=z32[:], in_=z_[:], func=AF.Copy), r=[zr_], w=[z32r])
                            kb.op("dve", lambda e, yo_=yo_, z32=z32: e.tensor_tensor(yo_[:], yo_[:], z32[:], ALU.mult), r=[z32r, yor], w=[yor])
                            q5, q5r = sq5.get()
                            kb.op("dve", lambda e, q5=q5, yo_=yo_: e.tensor_tensor(q5[:], yo_[:], yo_[:], ALU.mult), r=[yor], w=[q5r])
                            s_, sr_ = s1.get()
                            kb.op("dve", lambda e, s_=s_, q5=q5: e.tensor_reduce(s_[:], q5[:], AX.X, ALU.add), r=[q5r], w=[sr_])
                            rsqrt(s_[:], s_[:], 512 * C.EPS, [sr_], [sr_])
                            yb_, ybr = ynb.get()
                            kb.op("dve", lambda e, yb_=yb_, yo_=yo_, s_=s_: e.scalar_tensor_tensor(yb_[:], yo_[:], s_[:, 0:1], g512, ALU.mult, ALU.mult), r=[yor, sr_, R_pf], w=[ybr])
                            ptp, ptr = psum()
                            pb = ptp[:, :].bitcast(BF16)
                            for k in range(4):
                                kb.op("pe", lambda e, pb=pb, k=k, yb_=yb_: e.transpose(pb[:, k * 128:(k + 1) * 128], yb_[:, k * 128:(k + 1) * 128], identb[:]), r=[ybr, R_cst], w=[ptr])
                            yT_, yTr = yT.get()
                            kb.op("act", lambda e, yT_=yT_, pb=pb: e.activation(out=yT_[:], in_=pb[:, 0:512].rearrange("p (k t) -> p k t", t=128), func=AF.Copy), r=[ptr], w=[yTr])
                            kb.dma(bsd[:, :, c0:c0 + 128], yT_[:], r=[yTr], w=[R_bsd])
                    if C.sgl < 4:
                        continue
                    xc_, xcr = xdc.get()
                    kb.op("dve", lambda e, xc_=xc_, xd32=xd32, lt_=lt_: e.tensor_tensor(xc_[:].rearrange("p (h q) -> p h q", q=64), xd32[:].rearrange("p (h q) -> p h q", q=64), lt_[:, :, dcol:dcol + 1].broadcast_to([128, 8, 64]), ALU.mult), r=[xd32r, ltr], w=[xcr])
                    pS_, pSr = psum()
                    for g in range(2):
                        kb.op("pe", lambda e, pS_=pS_, g=g, bm_=bm_, xc_=xc_: e.matmul(pS_[:, g * 256:(g + 1) * 256], bm_[:, :], xc_[:, g * 256:(g + 1) * 256], start=True, stop=True), r=[bmr, xcr], w=[pSr])
                    for g in range(2):
                        rows = slice(g * 64, g * 64 + 64)
                        kb.op("dve", lambda e, g=g, rows=rows, da_=da_: e.tensor_tensor(hs[rows, :].rearrange("p (h q) -> p h q", q=64), hs[rows, :].rearrange("p (h q) -> p h q", q=64), da_[rows, 8 + 4 * g:12 + 4 * g].unsqueeze(2).broadcast_to([64, 4, 64]), ALU.mult), r=[dar, R_hs, R_hsb], w=[R_hs])
                        kb.op("dve", lambda e, g=g, rows=rows, pS_=pS_: e.tensor_tensor(hs[rows, :], hs[rows, :], pS_[rows, g * 256:(g + 1) * 256], ALU.add), r=[pSr, R_hs], w=[R_hs])

        mark("SH1")
        with stage_scope() as st:
            stg = Rot(st, "wstg", [128, 128], F32, 4)
            hbk = Rot(st, "hbk", [128, DC, 512], BF16)
            atb = Rot(st, "atb", [64, 8, 512], BF16)
            brb = Rot(st, "brb", [128, 3, 4, 512], BF16)
            sg = Rot(st, "sg", [128, 512], F32, 2)
            acc = Rot(st, "acc", [128, 512], F32, 2)
            tq = Rot(st, "tq", [128, 512], F32, 2)
            mo = Rot(st, "mo", [128, 512], BF16, 2)
            WG, WA, WB = [], [], []
            for dc in range(DC):
                wg = [load_w(st, "wg%d_%d" % (k, dc), wview(w_in, li, OFF["gl"] + k * D + dc * 128, 128), [128, DC, 128], stg) for k in range(4)]
                wa = load_w(st, "wba_%d" % dc, w_branch[li, 0, :, dc * 128:(dc + 1) * 128].rearrange("(h p) n -> p h n", p=64), [64, 8, 128], stg)
                wb = [load_w(st, "wb%d_%d" % (k, dc), w_branch[li, k, :, dc * 128:(dc + 1) * 128].rearrange("(c p) n -> p c n", p=128), [128, 4, 128], stg) for k in range(1, 4)]
                WG.append(wg); WA.append(wa); WB.append(wb)
            for seg, c0, n, f, l_ in blocks(512):
                if last and seg == 1:
                    continue
                ht, hr = hbk.get()
                kb.dma(ht[:, :, 0:n], hd[:, :, C.ext(c0):C.ext(c0) + n], r=[R_hd], w=[hr])
                at_, atr = atb.get()
                kb.dma(at_[:, :, 0:n], attd[:, :, c0:c0 + n], r=[R_attd], w=[atr])
                br_, brr = brb.get()
                for k, (src, rr) in enumerate(((bcd, R_bcd), (bpd, R_bpd), (bsd, R_bsd))):
                    kb.dma(br_[:, k, :, 0:n], src[:, :, c0:c0 + n], r=[rr], w=[brr])
                for dc in range(DC):
                    ac_, acr = acc.get()
                    for k in range(4):
                        pg, pgr = psum()
                        pp, ppr = psum()
                        wgk, wgkr = WG[dc][k]
                        for c in range(DC):
                            kb.op("pe", lambda e, wgk=wgk, c=c, pg=pg, ht=ht: e.matmul(pg[:, 0:n], wgk[:, c, :], ht[:, c, 0:n], start=(c == 0), stop=(c == DC - 1)), r=[hr, wgkr], w=[pgr])
                        if k == 0:
                            wa, war = WA[dc]
                            for h in range(8):
                                kb.op("pe", lambda e, wa=wa, h=h, pp=pp, at_=at_: e.matmul(pp[:, 0:n], wa[:, h, :], at_[:, h, 0:n], start=(h == 0), stop=(h == 7)), r=[atr, war], w=[ppr])
                        else:
                            wbk, wbkr = WB[dc][k - 1]
                            for c in range(4):
                                kb.op("pe", lambda e, wbk=wbk, k=k, c=c, pp=pp, br_=br_: e.matmul(pp[:, 0:n], wbk[:, c, :], br_[:, k - 1, c, 0:n], start=(c == 0), stop=(c == 3)), r=[brr, wbkr], w=[ppr])
                        s_, sr_ = sg.get()
                        kb.op("act", lambda e, s_=s_, pg=pg: e.activation(out=s_[:, 0:n], in_=pg[:, 0:n], func=AF.Sigmoid), r=[pgr], w=[sr_])
                        if k == 0:
                            kb.op("dve", lambda e, ac_=ac_, s_=s_, pp=pp: e.tensor_tensor(ac_[:, 0:n], s_[:, 0:n], pp[:, 0:n], ALU.mult), r=[sr_, ppr], w=[acr])
                        else:
                            t_, tr_ = tq.get()
                            kb.op("dve", lambda e, t_=t_, s_=s_, pp=pp: e.tensor_tensor(t_[:, 0:n], s_[:, 0:n], pp[:, 0:n], ALU.mult), r=[sr_, ppr], w=[tr_])
                            kb.op("pool", lambda e, ac_=ac_, t_=t_: e.tensor_tensor(ac_[:, 0:n], ac_[:, 0:n], t_[:, 0:n], ALU.add), r=[tr_, acr], w=[acr])
                    m_, mr_ = mo.get()
                    kb.op("act", lambda e, m_=m_, ac_=ac_: e.activation(out=m_[:, 0:n], in_=ac_[:, 0:n], func=AF.Copy), r=[acr], w=[mr_])
                    kb.dma(md[:, dc, c0:c0 + n], m_[:, 0:n], r=[mr_], w=[R_md])

        mark("SH2")
        with stage_scope() as st:
            stg = Rot(st, "wstg", [128, D], F32)
            wo, wor = load_w(st, "wo", wview(w_out, li, 0, D), [128, DC, D], stg)
            rots = norm_rots(st, 512)
            xb = Rot(st, "xb", [128, DC, 512], F32)
            mb = Rot(st, "mb", [128, DC, 512], BF16)
            for seg, c0, n, f, l_ in blocks(512):
                if last and seg == 1:
                    continue
                xt, xr = xb.get()
                kb.dma(xt[:, :, 0:n], xd[:, :, c0:c0 + n], r=[R_xd], w=[xr])
                mt, mr = mb.get()
                kb.dma(mt[:, :, 0:n], md[:, :, c0:c0 + n], r=[R_md], w=[mr])
                for dc in range(DC):
                    py, pyr = psum()
                    for c in range(DC):
                        kb.op("pe", lambda e, c=c, dc=dc, py=py, mt=mt: e.matmul(py[:, 0:n], wo[:, c, dc * 128:(dc + 1) * 128], mt[:, c, 0:n], start=(c == 0), stop=(c == DC - 1)), r=[mr, wor], w=[pyr])
                    kb.op("dve", lambda e, dc=dc, py=py, xt=xt, seg=seg: e.scalar_tensor_tensor(xt[:, dc, 0:n], py[:, 0:n], AB[:, 2, dc, seg:seg + 1], xt[:, dc, 0:n], ALU.mult, ALU.add), r=[pyr, R_mod, xr], w=[xr])
                kb.dma(xd[:, :, c0:c0 + n], xt[:, :, 0:n], r=[xr], w=[R_xd])
                norm_block(rots, xt, xr, c0, n, seg, 3, 4)

        mark("SI")
        with stage_scope() as st:
            SBW = min(C.SBW, T)
            GJ = 4
            stg = Rot(st, "wstg", [128, 1024], F32, 3)
            hx = st.enter_context(nc.sbuf_tensor(U("hx"), [128, DC, SBW + 2], BF16))
            R_hx = Reg("hx")
            y2 = st.enter_context(nc.sbuf_tensor(U("y2"), [128, DC, SBW], F32))
            R_y2 = Reg("y2")
            actT = st.enter_context(nc.sbuf_tensor(U("actT"), [128, GJ, SBW], BF16))
            R_act = Reg("actT")
            upr = Rot(st, "upr", [128, 2, SBW + 2], BF16, 1)
            cg_ = Rot(st, "cgt", [128, SBW], F32, 1)
            cv_ = Rot(st, "cvt", [128, SBW], F32, 1)
            wup = Rot(st, "wup", [128, DC, 256], BF16, 2)
            wdn = Rot(st, "wdn", [128, GJ, D], BF16, 1)
            xb = Rot(st, "xb1", [128, 512], F32, 3)
            sblocks = []
            R_xd2, R_xd3 = Reg("xd2"), Reg("xd3")
            for seg, s0, ln in C.segs():
                if last and seg == 1:
                    continue
                for c in range(0, ln, SBW):
                    sblocks.append((seg, s0 + c, min(SBW, ln - c)))
            for seg, s0, m in sblocks:
                kb.dma(hx[:, :, 0:m + 2], hd[:, :, C.ext(s0) - 1:C.ext(s0) + m + 1], r=[R_hd], w=[R_hx])
                subs = [(c, min(512, m + 2 - c)) for c in range(0, m + 2, 512)]
                subs2 = [(c, min(512, m - c)) for c in range(0, m, 512)]
                for j0 in range(0, FC, GJ):
                    gj = min(GJ, FC - j0)
                    for jj in range(gj):
                        j = j0 + jj
                        wu, wur = wup.get()
                        for k, col in enumerate((j * 128, DFF + j * 128)):
                            s_t, s_r = stg.get()
                            kb.dma(s_t[:, 0:DC * 128].rearrange("p (c n) -> p c n", n=128), wview(w_up, li, col, 128), w=[s_r])
                            kb.op("pool" if k else "dve", lambda e, wu=wu, k=k, s_t=s_t: e.tensor_copy(wu[:, :, k * 128:(k + 1) * 128], s_t[:, 0:DC * 128].rearrange("p (c n) -> p c n", n=128)), r=[s_r], w=[wur])
                        up_, upr_ = upr.get()
                        for k in range(2):
                            for (c, nn) in subs:
                                pu_, pur = psum()
                                for cc in range(DC):
                                    kb.op("pe", lambda e, cc=cc, k=k, c=c, nn=nn, pu_=pu_, wu=wu: e.matmul(pu_[:, 0:nn], wu[:, cc, k * 128:(k + 1) * 128], hx[:, cc, c:c + nn], start=(cc == 0), stop=(cc == DC - 1)), r=[R_hx, wur], w=[pur])
                                kb.op("act", lambda e, k=k, c=c, nn=nn, pu_=pu_, up_=up_: e.activation(out=up_[:, k, c:c + nn], in_=pu_[:, 0:nn], func=AF.Copy), r=[pur], w=[upr_])
                        outs = []
                        for k, (rot, eng) in enumerate(((cg_, "dve"), (cv_, "dve"))):
                            t_, tr_ = rot.get()
                            w3 = pf[:, P_FC + 3 * (k * FC + j):P_FC + 3 * (k * FC + j) + 3]
                            kb.op(eng, lambda e, t_=t_, up_=up_, k=k, w3=w3: e.tensor_scalar(t_[:, 0:m], up_[:, k, 1:m + 1], w3[:, 1:2], None, ALU.mult), r=[upr_, R_pf], w=[tr_])
                            kb.op(eng, lambda e, t_=t_, up_=up_, k=k, w3=w3: e.scalar_tensor_tensor(t_[:, 0:m], up_[:, k, 0:m], w3[:, 0:1], t_[:, 0:m], ALU.mult, ALU.add), r=[upr_, R_pf, tr_], w=[tr_])
                            kb.op(eng, lambda e, t_=t_, up_=up_, k=k, w3=w3: e.scalar_tensor_tensor(t_[:, 0:m], up_[:, k, 2:m + 2], w3[:, 2:3], t_[:, 0:m], ALU.mult, ALU.add), r=[upr_, R_pf, tr_], w=[tr_])
                            outs.append((t_, tr_))
                        (tg, tgr), (tv, tvr) = outs
                        kb.op("act", lambda e, tg=tg: e.activation(out=tg[:, 0:m], in_=tg[:, 0:m], func=AF.Silu), r=[tgr], w=[tgr])
                        kb.op("dve", lambda e, jj=jj, tg=tg, tv=tv: e.tensor_tensor(actT[:, jj, 0:m], tg[:, 0:m], tv[:, 0:m], ALU.mult), r=[tgr, tvr], w=[R_act])
                    wd, wdr = wdn.get()
                    for jj in range(gj):
                        s_t, s_r = stg.get()
                        kb.dma(s_t[:, 0:D], w_down[li, (j0 + jj) * 128:(j0 + jj + 1) * 128, :], w=[s_r])
                        kb.op("pool", lambda e, wd=wd, jj=jj, s_t=s_t: e.tensor_copy(wd[:, jj, :], s_t[:, 0:D]), r=[s_r], w=[wdr])
                    for dc in range(DC):
                        for (c, nn) in subs2:
                            py, pyr = psum()
                            for jj in range(gj):
                                kb.op("pe", lambda e, jj=jj, dc=dc, c=c, nn=nn, py=py, wd=wd: e.matmul(py[:, 0:nn], wd[:, jj, dc * 128:(dc + 1) * 128], actT[:, jj, c:c + nn], start=(jj == 0), stop=(jj == gj - 1)), r=[R_act, wdr], w=[pyr])
                            if j0 == 0:
                                kb.op("act", lambda e, dc=dc, c=c, nn=nn, py=py: e.activation(out=y2[:, dc, c:c + nn], in_=py[:, 0:nn], func=AF.Copy), r=[pyr], w=[R_y2])
                            else:
                                kb.op("dve", lambda e, dc=dc, c=c, nn=nn, py=py: e.tensor_tensor(y2[:, dc, c:c + nn], y2[:, dc, c:c + nn], py[:, 0:nn], ALU.add), r=[pyr, R_y2], w=[R_y2])
                for (c, nn) in subs2:
                    for dc in range(DC):
                        xt, xr = xb.get()
                        kb.dma(xt[:, 0:nn], xd[:, dc, s0 + c:s0 + c + nn], r=[R_xd2], w=[xr])
                        kb.op("dve", lambda e, dc=dc, c=c, nn=nn, xt=xt, seg=seg: e.scalar_tensor_tensor(xt[:, 0:nn], y2[:, dc, c:c + nn], AB[:, 5, dc, seg:seg + 1], xt[:, 0:nn], ALU.mult, ALU.add), r=[R_y2, R_mod, xr], w=[xr])
                        kb.dma(xd[:, dc, s0 + c:s0 + c + nn], xt[:, 0:nn], r=[xr], w=[R_xd3])

    mark("EPI")
    with stage_scope() as st:
        fn = st.enter_context(nc.sbuf_tensor(U("fn"), [128, D], F32))
        R_fn = Reg("fn")
        kb.dma(fn[:], fnorm_in[:, :], w=[R_fn])
        kb.op("dve", lambda e: e.tensor_scalar(fn[:], fn[:], float(math.sqrt(D)), None, ALU.mult), r=[R_fn], w=[R_fn])
        xb = Rot(st, "xb", [128, DC, 128], F32)
        xtm = Rot(st, "xtm", [128, D], F32)
        sqq = Rot(st, "sqq", [128, D], F32, 1)
        s1 = Rot(st, "s1", [128, 1], F32, 2)
        ob = Rot(st, "ob", [128, D], F32)
        for ti in range(C.LT):
            c0 = CTX + ti * 128
            xt, xr = xb.get()
            kb.dma(xt[:], xd[:, :, c0:c0 + 128], r=[R_xd], w=[xr])
            xm, xmr = xtm.get()
            for c4 in range(0, DC, 4):
                pst, psr = psum()
                nn = min(4, DC - c4)
                for c in range(c4, c4 + nn):
                    kb.op("pe", lambda e, pst=pst, xt=xt, c=c, c4=c4: e.transpose(pst[:, (c - c4) * 128:(c - c4 + 1) * 128], xt[:, c, :], identf), r=[xr, R_cst], w=[psr])
                kb.op("act", lambda e, pst=pst, xm=xm, c4=c4, nn=nn: e.activation(out=xm[:, c4 * 128:(c4 + nn) * 128], in_=pst[:, 0:nn * 128], func=AF.Copy), r=[psr], w=[xmr])
            q_, qr_ = sqq.get()
            kb.op("dve", lambda e, q_=q_, xm=xm: e.tensor_tensor(q_[:], xm[:], xm[:], ALU.mult), r=[xmr], w=[qr_])
            s_, sr_ = s1.get()
            kb.op("dve", lambda e, s_=s_, q_=q_: e.tensor_reduce(s_[:], q_[:], AX.X, ALU.add), r=[qr_], w=[sr_])
            rsqrt(s_[:], s_[:], D * C.EPS, [sr_], [sr_])
            o_, or_ = ob.get()
            kb.op("dve", lambda e, o_=o_, xm=xm, s_=s_: e.scalar_tensor_tensor(o_[:], xm[:], s_[:, 0:1], fn[:], ALU.mult, ALU.mult), r=[xmr, sr_, R_fn], w=[or_])
            kb.dma(out[ti * 128:(ti + 1) * 128, :], o_[:], r=[or_], w=[R_out])
    kb.full_barrier()
    cnt = kb.emit()
    es.close()
    return nc, cnt


def _fm(v, nchunk):
    return np.ascontiguousarray(np.asarray(v, np.float32).reshape(nchunk, 128).T)


def _rep(v):
    v = np.asarray(v, np.float32).reshape(1, -1)
    return np.ascontiguousarray(np.broadcast_to(v, (128, v.shape[1])))


def pack_inputs(cfg, inp, b):
    C = cfg
    L, DC, FC = C.DEPTH, C.DC, C.FC
    f32 = np.float32
    m = {}
    m["x"] = np.ascontiguousarray(inp["x"][b], dtype=f32)
    m["ctx"] = np.ascontiguousarray(inp["ctx"][b], dtype=f32)
    m["cond"] = np.ascontiguousarray(np.stack([_fm(inp["c"][b], DC), _fm(inp["c_ctx"], DC)], axis=-1))
    for k in ("w_mod", "w_in", "pool_w", "w_branch", "w_out", "w_up", "w_down"):
        m[k] = np.ascontiguousarray(inp[k], dtype=f32)
    pf, pt = [], []
    for l in range(L):
        cols = [_fm(inp["b_mod"][l], 6 * DC), _fm(inp["norm_mix"][l], DC), _fm(inp["norm_ffn"][l], DC)]
        cs = np.asarray(inp["conv_short"][l], f32)
        cols.append(np.ascontiguousarray(cs.reshape(3, 4, 128).transpose(2, 1, 0)).reshape(128, 12))
        cols.append(_fm(inp["pool_scale"][l], 4))
        sw = np.asarray(inp["ssd_conv_w"][l], f32)
        cols.append(np.ascontiguousarray(sw.reshape(3, 6, 128).transpose(2, 1, 0)).reshape(128, 18))
        cols.append(_fm(inp["ssd_conv_b"][l], 6))
        fc = np.asarray(inp["ffn_conv"][l], f32)
        cols.append(np.ascontiguousarray(fc.reshape(3, 2 * FC, 128).transpose(2, 1, 0)).reshape(128, 6 * FC))
        pf.append(np.concatenate(cols, axis=1))
        tcols = [_rep(np.tile(np.asarray(inp["q_norm"][l], f32), 8)), _rep(np.tile(np.asarray(inp["k_norm"][l], f32), 2)),
                 _rep(inp["ssd_norm"][l]), _rep(np.asarray(inp["ssd_dt_bias"][l], f32).reshape(-1)),
                 _rep(np.asarray(inp["ssd_a_log"][l], f32).reshape(-1)), _rep(inp["ssd_d"][l])]
        pt.append(np.concatenate(tcols, axis=1))
    m["pf"] = np.ascontiguousarray(np.stack(pf), dtype=f32)
    m["pt"] = np.ascontiguousarray(np.stack(pt), dtype=f32)
    m["fnorm"] = _rep(inp["final_norm"])
    t = np.arange(C.SEQ)
    row, col = (t // C.GRID_W).astype(f32), (t % C.GRID_W).astype(f32)
    inv = (np.float32(10000.0) ** (-np.arange(16, dtype=f32) / np.float32(16))).astype(f32)
    ang = np.stack([row[:, None] * inv, col[:, None] * inv], axis=1).astype(f32)
    tab = np.concatenate([np.cos(ang).reshape(C.SEQ, 32), np.sin(ang).reshape(C.SEQ, 32)], axis=1).astype(f32)
    m["rope"] = np.ascontiguousarray(tab.reshape(C.LT, 128, 64).transpose(1, 0, 2))
    invc = np.zeros((128, 4, 2, 16), f32)
    for g, w in enumerate((2, 4, 8, 16)):
        for seg, n in ((0, C.SEQ), (1, C.CTX)):
            for i in range(8):
                for (tt, slot) in ((i, i), (n - 8 + i, 8 + i)):
                    lo, hi = max(tt - w // 2, 0), min(tt + w // 2, n)
                    invc[:, g, seg, slot] = 1.0 / float(hi - lo)
    m["invc"] = invc
    k = np.arange(128)
    cst = np.zeros((128, 7, 128), f32)
    cst[:, 0, :] = np.eye(128)
    cst[:, 1, :] = (k[:, None] > k[None, :])
    cst[:, 2, :] = (k[:, None] <= k[None, :])
    cst[:, 3, :] = (k[None, :] >= k[:, None])
    cst[:, 4, :] = (k[:, None] < k[None, :])
    cst[:, 5, :] = (k[:, None] >= k[None, :])
    cst[:, 6, :] = (k[:, None] >= k[None, :])
    m["cst"] = cst
    return m


_CACHE = {}


def kernel(**inputs):
    cfg = Cfg()
    if "nc" not in _CACHE:
        _CACHE["nc"] = build_program(cfg)[0]
    nc = _CACHE["nc"]
    nb = inputs["x"].shape[0]
    in_maps = [pack_inputs(cfg, inputs, b) for b in range(nb)]
    res = run_bass_kernel_spmd(nc, in_maps, core_ids=list(range(nb)))
    return np.stack([np.asarray(res.results[b]["out"], dtype=np.float32) for b in range(nb)], axis=0)
```
